# Optimizing a Trainium2 kernel written in Bass

```python
import math
import jax, jax.numpy as jnp
from jax import lax
import numpy as np

D_MODEL = 1024
BATCH = 8
SEQ = 2048
DEPTH = 2
DEC_BATCH = 128
DEC_SEQ = 1
PAST_LEN = 16384
PAGE_SIZE = 128

N_MIXERS = 2
N_POOL_LAYERS = (DEPTH + 1) // 2
N_DN_LAYERS = DEPTH // 2
POOL_WINDOWS = (2, 4, 8, 16)
N_POOL_GROUPS = len(POOL_WINDOWS)
POOL_GD = D_MODEL // N_POOL_GROUPS
POOL_BUF = max(POOL_WINDOWS) - 1
DN_HEAD_DIM = 128
DN_HEADS = D_MODEL // DN_HEAD_DIM
DN_DIM = DN_HEADS * DN_HEAD_DIM
DN_CONV = 4
DN_CHUNK = 64
DT_MIN = 0.001
DT_MAX = 0.1
D_FF = ((8 * D_MODEL // 3 + 127) // 128) * 128
FFN_CONV = 3
NORM_EPS = 1e-6

kernel_name = "hybrid_pool_gdn_convffn_step"


def rms_norm(x, w):
    xf = x.astype(jnp.float32)
    y = xf * lax.rsqrt(jnp.mean(xf * xf, axis=-1, keepdims=True) + NORM_EPS)
    return (y * w.astype(jnp.float32)).astype(x.dtype)


def l2_norm(x):
    xf = x.astype(jnp.float32)
    return xf * lax.rsqrt(jnp.sum(xf * xf, axis=-1, keepdims=True) + NORM_EPS)


def causal_depthwise_conv(xx, w, t):
    out = xx[:, 0:t] * w[0]
    for j in range(1, w.shape[0]):
        out = out + xx[:, j:j + t] * w[j]
    return out


def pool_mixer(xn, buf, pos0, w_grp, scale):
    b, t, d = xn.shape
    xx = jnp.concatenate([buf.astype(xn.dtype), xn], axis=1)
    cs = jnp.cumsum(xx.astype(jnp.float32), axis=1)
    cs = jnp.pad(cs, ((0, 0), (1, 0), (0, 0)))
    csg = cs.reshape(b, POOL_BUF + 1 + t, N_POOL_GROUPS, POOL_GD)
    xg = xn.astype(jnp.float32).reshape(b, t, N_POOL_GROUPS, POOL_GD)
    pos = pos0 + jnp.arange(t)
    diffs = []
    for g, w in enumerate(POOL_WINDOWS):
        s = csg[:, POOL_BUF + 1:POOL_BUF + 1 + t, g] - csg[:, POOL_BUF + 1 - w:POOL_BUF + 1 - w + t, g]
        cnt = jnp.minimum(w, pos + 1).astype(jnp.float32)[None, :, None]
        diffs.append(s / cnt - xg[:, :, g])
    dg = jnp.stack(diffs, axis=2)
    y = jnp.einsum('btgc,gcd->btgd', dg, w_grp.astype(jnp.float32)).reshape(b, t, d)
    out = (y * scale.astype(jnp.float32)).astype(xn.dtype)
    return out, xx[:, -POOL_BUF:]


def gated_delta_rule(q, k, v, g, beta, s0):
    b, t, h, _ = q.shape
    dv = v.shape[-1]
    c = min(DN_CHUNK, t)
    pad = (-t) % c
    nc = (t + pad) // c

    def chunks(a):
        a = jnp.pad(a.astype(jnp.float32), [(0, 0), (0, pad)] + [(0, 0)] * (a.ndim - 2))
        a = a.reshape((b, nc, c) + a.shape[2:])
        return jnp.moveaxis(a, 3, 2)

    q, k, v, g, beta = chunks(q), chunks(k), chunks(v), chunks(g), chunks(beta)
    gc = jnp.cumsum(g, axis=-1)
    incl = jnp.tril(jnp.ones((c, c), dtype=bool))
    strict = jnp.tril(jnp.ones((c, c), dtype=bool), k=-1)
    decay = jnp.exp(jnp.where(incl, gc[..., :, None] - gc[..., None, :], -jnp.inf))
    kb = k * beta[..., None]
    vb = v * beta[..., None]
    a_mat = jnp.where(strict, jnp.einsum('bnhid,bnhjd->bnhij', kb, k) * decay, 0.0) + jnp.eye(c, dtype=jnp.float32)
    rhs = jnp.concatenate([vb, kb * jnp.exp(gc)[..., None]], axis=-1)
    sol = lax.linalg.triangular_solve(a_mat, rhs, left_side=True, lower=True, unit_diagonal=True)
    u_in, w_in = sol[..., :dv], sol[..., dv:]
    qk = jnp.einsum('bnhid,bnhjd->bnhij', q, k) * decay
    q_dec = q * jnp.exp(gc)[..., None]
    g_last = gc[..., -1]
    k_dec = k * jnp.exp(g_last[..., None] - gc)[..., None]

    def step(s, xs):
        u_c, w_c, qk_c, qd_c, kd_c, gl_c = xs
        u = u_c - jnp.einsum('bhck,bhkv->bhcv', w_c, s)
        o = jnp.einsum('bhck,bhkv->bhcv', qd_c, s) + jnp.einsum('bhij,bhjv->bhiv', qk_c, u)
        s = s * jnp.exp(gl_c)[..., None, None] + jnp.einsum('bhck,bhcv->bhkv', kd_c, u)
        return s, o

    xs = tuple(jnp.moveaxis(a, 1, 0) for a in (u_in, w_in, qk, q_dec, k_dec, g_last))
    s_final, o = lax.scan(step, s0.astype(jnp.float32), xs)
    o = jnp.transpose(o, (1, 0, 3, 2, 4)).reshape(b, nc * c, h, dv)[:, :t]
    return o, s_final


def deltanet_mixer(xn, conv_buf, s0, w_in, conv_w, a_log, dt_bias, o_norm_w, w_out):
    b, t, _ = xn.shape
    proj = xn @ w_in
    qkv, z, a, bb = jnp.split(proj, [3 * DN_DIM, 4 * DN_DIM, 4 * DN_DIM + DN_HEADS], axis=-1)
    qkv_all = jnp.concatenate([conv_buf.astype(qkv.dtype), qkv], axis=1)
    qkv_c = jax.nn.silu(causal_depthwise_conv(qkv_all, conv_w, t))
    q, k, v = jnp.split(qkv_c, 3, axis=-1)
    q = l2_norm(q.reshape(b, t, DN_HEADS, DN_HEAD_DIM)) * (DN_HEAD_DIM ** -0.5)
    k = l2_norm(k.reshape(b, t, DN_HEADS, DN_HEAD_DIM))
    v = v.reshape(b, t, DN_HEADS, DN_HEAD_DIM)
    beta = jax.nn.sigmoid(bb.astype(jnp.float32))
    g = -jnp.exp(a_log.astype(jnp.float32)) * jax.nn.softplus(a.astype(jnp.float32) + dt_bias.astype(jnp.float32))
    o, s_new = gated_delta_rule(q, k, v, g, beta, s0)
    o = rms_norm(o, o_norm_w) * jax.nn.silu(z.astype(jnp.float32).reshape(b, t, DN_HEADS, DN_HEAD_DIM))
    out = o.reshape(b, t, DN_DIM).astype(xn.dtype) @ w_out
    return out, qkv_all[:, -(DN_CONV - 1):], s_new.astype(s0.dtype)


def conv_ffn(xn, buf, w_up, conv_w, conv_b, w_down):
    t = xn.shape[1]
    hh = jnp.concatenate([buf.astype(xn.dtype), xn @ w_up], axis=1)
    c = causal_depthwise_conv(hh, conv_w, t) + conv_b
    gate, val = jnp.split(c, 2, axis=-1)
    return (jax.nn.silu(gate) * val) @ w_down, hh[:, -(FFN_CONV - 1):]


def trunk(x, pool_buf, dn_conv, dn_ssm, ffn_conv, pos0, params):
    (norm1_w, norm2_w, final_norm_w, pool_w, pool_scale, dn_w_in, dn_conv_w, dn_a_log,
     dn_dt_bias, dn_o_norm_w, dn_w_out, ffn_w_up, ffn_conv_w, ffn_conv_b, ffn_w_down) = params
    new_pool, new_dnc, new_dns, new_ffn = [], [], [], []
    for i in range(DEPTH):
        h = rms_norm(x, norm1_w[i])
        j = i // N_MIXERS
        if i % N_MIXERS == 0:
            out, nb = pool_mixer(h, pool_buf[j], pos0, pool_w[j], pool_scale[j])
            new_pool.append(nb)
        else:
            out, ncb, ns = deltanet_mixer(h, dn_conv[j], dn_ssm[j], dn_w_in[j], dn_conv_w[j], dn_a_log[j],
                                          dn_dt_bias[j], dn_o_norm_w[j], dn_w_out[j])
            new_dnc.append(ncb)
            new_dns.append(ns)
        x = x + out
        out, nf = conv_ffn(rms_norm(x, norm2_w[i]), ffn_conv[i], ffn_w_up[i], ffn_conv_w[i], ffn_conv_b[i], ffn_w_down[i])
        x = x + out
        new_ffn.append(nf)
    y = rms_norm(x, final_norm_w)
    return y, jnp.stack(new_pool), jnp.stack(new_dnc), jnp.stack(new_dns), jnp.stack(new_ffn)


def setup_inputs(seed: int = 0) -> dict:
    key = jax.random.key(seed)
    ks = jax.random.split(key, 22)
    f32 = jnp.float32

    def nrm(k, shape, s):
        return jax.random.normal(k, shape, f32) * s

    dn_in_cols = 4 * DN_DIM + 2 * DN_HEADS
    dt = jnp.exp(jax.random.uniform(ks[10], (N_DN_LAYERS, DN_HEADS), f32)
                 * (math.log(DT_MAX) - math.log(DT_MIN)) + math.log(DT_MIN))
    return {
        'x_prompt': nrm(ks[0], (BATCH, SEQ, D_MODEL), 1.0),
        'x_sample': nrm(ks[1], (DEC_BATCH, DEC_SEQ, D_MODEL), 1.0),
        'state_pool_buf': nrm(ks[2], (N_POOL_LAYERS, DEC_BATCH, POOL_BUF, D_MODEL), 1.0),
        'state_dn_conv': nrm(ks[3], (N_DN_LAYERS, DEC_BATCH, DN_CONV - 1, 3 * DN_DIM), 1.0),
        'state_dn_ssm': nrm(ks[4], (N_DN_LAYERS, DEC_BATCH, DN_HEADS, DN_HEAD_DIM, DN_HEAD_DIM), DN_HEAD_DIM ** -0.5),
        'state_ffn_conv': nrm(ks[5], (DEPTH, DEC_BATCH, FFN_CONV - 1, 2 * D_FF), 1.0),
        'norm1_w': 1.0 + nrm(ks[6], (DEPTH, D_MODEL), 0.02),
        'norm2_w': 1.0 + nrm(ks[7], (DEPTH, D_MODEL), 0.02),
        'final_norm_w': 1.0 + nrm(ks[8], (D_MODEL,), 0.02),
        'pool_w': nrm(ks[9], (N_POOL_LAYERS, N_POOL_GROUPS, POOL_GD, POOL_GD), POOL_GD ** -0.5),
        'pool_scale': 1.0 + nrm(ks[11], (N_POOL_LAYERS, D_MODEL), 0.1),
        'dn_w_in': nrm(ks[12], (N_DN_LAYERS, D_MODEL, dn_in_cols), D_MODEL ** -0.5),
        'dn_conv_w': nrm(ks[13], (N_DN_LAYERS, DN_CONV, 3 * DN_DIM), DN_CONV ** -0.5),
        'dn_a_log': jnp.log(jax.random.uniform(ks[14], (N_DN_LAYERS, DN_HEADS), f32, 1.0, 16.0)),
        'dn_dt_bias': dt + jnp.log(-jnp.expm1(-dt)),
        'dn_o_norm_w': 1.0 + nrm(ks[15], (N_DN_LAYERS, DN_HEAD_DIM), 0.02),
        'dn_w_out': nrm(ks[16], (N_DN_LAYERS, DN_DIM, D_MODEL), DN_DIM ** -0.5),
        'ffn_w_up': nrm(ks[17], (DEPTH, D_MODEL, 2 * D_FF), D_MODEL ** -0.5),
        'ffn_conv_w': nrm(ks[18], (DEPTH, FFN_CONV, 2 * D_FF), FFN_CONV ** -0.5),
        'ffn_conv_b': nrm(ks[19], (DEPTH, 2 * D_FF), 0.01),
        'ffn_w_down': nrm(ks[20], (DEPTH, D_FF, D_MODEL), D_FF ** -0.5),
    }


def reference(x_prompt, x_sample, state_pool_buf, state_dn_conv, state_dn_ssm, state_ffn_conv,
              norm1_w, norm2_w, final_norm_w, pool_w, pool_scale, dn_w_in, dn_conv_w, dn_a_log,
              dn_dt_bias, dn_o_norm_w, dn_w_out, ffn_w_up, ffn_conv_w, ffn_conv_b, ffn_w_down):
    params = (norm1_w, norm2_w, final_norm_w, pool_w, pool_scale, dn_w_in, dn_conv_w, dn_a_log,
              dn_dt_bias, dn_o_norm_w, dn_w_out, ffn_w_up, ffn_conv_w, ffn_conv_b, ffn_w_down)
    bp, dt = x_prompt.shape[0], x_prompt.dtype
    zero_pool = jnp.zeros((N_POOL_LAYERS, bp, POOL_BUF, D_MODEL), dt)
    zero_dnc = jnp.zeros((N_DN_LAYERS, bp, DN_CONV - 1, 3 * DN_DIM), dt)
    zero_dns = jnp.zeros((N_DN_LAYERS, bp, DN_HEADS, DN_HEAD_DIM, DN_HEAD_DIM), jnp.float32)
    zero_ffn = jnp.zeros((DEPTH, bp, FFN_CONV - 1, 2 * D_FF), dt)
    y_prompt, pool_p, dnc_p, dns_p, ffn_p = trunk(x_prompt, zero_pool, zero_dnc, zero_dns, zero_ffn, 0, params)
    y_sample, pool_s, dnc_s, dns_s, ffn_s = trunk(x_sample, state_pool_buf, state_dn_conv, state_dn_ssm,
                                                  state_ffn_conv, PAST_LEN, params)
    return (y_prompt, y_sample, pool_p, pool_s, dnc_p, dnc_s, dns_p, dns_s, ffn_p, ffn_s)
```

```python
import contextlib
import heapq
import numpy as np
import concourse.bass as bass
import concourse.mybir as mybir
from concourse.bass_utils import run_bass_kernel_spmd

F32 = mybir.dt.float32
BF16 = mybir.dt.bfloat16
ALU = mybir.AluOpType
AF = mybir.ActivationFunctionType
AX = mybir.AxisListType


class Buf:
    __slots__ = ("name", "w", "r")

    def __init__(self, name=""):
        self.name = name
        self.w = None
        self.r = []


class Node:
    __slots__ = ("eng", "fn", "sem", "inc", "deps", "dur", "lat", "idx", "tick", "users_x", "phase", "fin", "nd", "outs")

    def __init__(self, eng, fn, sem, inc, dur, lat, idx, phase):
        self.eng = eng
        self.fn = fn
        self.sem = sem
        self.inc = inc
        self.deps = []
        self.dur = dur
        self.lat = lat
        self.idx = idx
        self.tick = None
        self.users_x = False
        self.phase = phase
        self.fin = 0.0
        self.nd = 0
        self.outs = []


class Sched:
    ENG = ("pe", "act", "dve", "pool", "sp")
    SEM_LAT = 0.6
    REORDER = True
    CP = False

    def __init__(self, nc, es):
        self.nc = nc
        self.es = es
        self.nodes = []
        self.cnt = {e: 0 for e in self.ENG}
        self.seen = {e: {} for e in self.ENG}
        self.dma_cnt = {}
        self.last_dma = {}
        self.sem = {}
        self.phase = 0
        self.tail = {e: [] for e in self.ENG}
        self.nidx = 0
        for e in self.ENG:
            self._sem(e)

    def _sem(self, k):
        if k not in self.sem:
            nm = "s_" + "_".join(str(x) for x in (k if isinstance(k, tuple) else (k,)))
            self.sem[k] = self.es.enter_context(self.nc.semaphore(nm))
        return self.sem[k]

    def _add(self, node, reads, writes):
        ph = self.phase
        deps = node.deps
        for b in reads:
            w = b.w
            if w is not None and w.phase == ph:
                deps.append(w)
        for b in writes:
            w = b.w
            if w is not None and w.phase == ph:
                deps.append(w)
            for r in b.r:
                if r.phase == ph:
                    deps.append(r)
        for b in reads:
            b.r.append(node)
        for b in writes:
            b.w = node
            b.r = []
        self.nodes.append(node)

    def op(self, eng, fn, reads=(), writes=(), sig=True, dur=0.3):
        self.nidx += 1
        n = Node(eng, fn, eng, 1, dur, dur, self.nidx, self.phase)
        self._add(n, reads, writes)

    def dma(self, q, fn, semkey, reads=(), writes=(), nbytes=65536):
        self._sem(semkey)
        self.nidx += 1
        issue = 0.7 if q == "pool" else 0.15
        n = Node(q, fn, semkey, 16, issue, 2.2 + nbytes / 150e3, self.nidx, self.phase)
        prev = self.last_dma.get(semkey)
        if prev is not None and prev.phase == self.phase:
            n.deps.append(prev)
        self.last_dma[semkey] = n
        self._add(n, reads, writes)

    def _schedule(self, nodes):
        if not self.REORDER:
            order = {e: [] for e in self.ENG}
            for n in nodes:
                order[n.eng].append(n)
            return order
        for n in nodes:
            n.deps = list({id(d): d for d in n.deps}.values())
            n.nd = len(n.deps)
            n.outs = []
        for n in nodes:
            for d in n.deps:
                d.outs.append(n)
        lat0 = getattr(self, '_lat', self.SEM_LAT)
        cpl = {}
        for n in reversed(nodes):
            m = 0.0
            for o in n.outs:
                v = cpl[id(o)] + lat0
                if v > m:
                    m = v
            cpl[id(n)] = n.lat + m
        if getattr(self, '_cp', self.CP):
            for n in nodes:
                n.idx = (-cpl[id(n)], n.idx)
        free = {e: 0.0 for e in self.ENG}
        fut = {e: [] for e in self.ENG}
        now = {e: [] for e in self.ENG}
        rt = {}
        for n in nodes:
            if n.nd == 0:
                heapq.heappush(fut[n.eng], (0.0, n.idx, n))
        order = {e: [] for e in self.ENG}
        left = len(nodes)
        lat = getattr(self, '_lat', self.SEM_LAT)
        while left:
            best = None
            for e in self.ENG:
                f, fu, nw = free[e], fut[e], now[e]
                while fu and fu[0][0] <= f:
                    _, i, n = heapq.heappop(fu)
                    heapq.heappush(nw, (i, n))
                if nw:
                    cand = (f, nw[0][0], e, 0)
                elif fu:
                    cand = (fu[0][0], fu[0][1], e, 1)
                else:
                    continue
                if best is None or cand < best:
                    best = cand
            start, _, e, src = best
            if src == 0:
                _, n = heapq.heappop(now[e])
            else:
                _, _, n = heapq.heappop(fut[e])
            order[e].append(n)
            free[e] = start + n.dur
            n.fin = start + n.lat
            left -= 1
            for o in n.outs:
                o.nd -= 1
                r = rt.get(id(o), 0.0)
                t = n.fin + (0.0 if (n.eng == "pe" and o.eng == "pe") else lat)
                if t > r:
                    r = t
                rt[id(o)] = r
                if o.nd == 0:
                    heapq.heappush(fut[o.eng], (r, o.idx, o))
        return order

    def barrier(self):
        pass

    def finish(self, eng="sp"):
        pass

    def emit(self, final=False, lat=None, cp=None):
        self._cp = self.CP if cp is None else cp
        if lat is not None:
            self._lat = lat
        else:
            self._lat = self.SEM_LAT
        nodes = self.nodes
        self.nodes = []
        order = self._schedule(nodes)
        for n in nodes:
            for d in n.deps:
                if not (d.eng == "pe" and n.eng == "pe"):
                    d.users_x = True
        prog = {e: [] for e in self.ENG}
        for e in self.ENG:
            lst = order[e]
            last_sig = None
            for n in lst:
                if n.sem == e:
                    if n.eng != "pe" or n.users_x:
                        self.cnt[e] += 1
                        n.tick = self.cnt[e]
                        last_sig = n
                    else:
                        n.tick = None
                else:
                    self.dma_cnt[n.sem] = self.dma_cnt.get(n.sem, 0) + 16
                    n.tick = self.dma_cnt[n.sem]
            if e == "pe":
                for n in reversed(lst):
                    if n.sem == e:
                        if n.tick is None:
                            self.cnt[e] += 1
                            n.tick = self.cnt[e]
                        break
        nxt = None
        for n in reversed(order["pe"]):
            if n.tick is not None:
                nxt = n.tick
            else:
                n.tick = -(nxt if nxt is not None else 0)
        for e in self.ENG:
            seen = self.seen[e]
            for n in order[e]:
                waits = []
                for d in n.deps:
                    if d.eng == "pe" and e == "pe":
                        continue
                    k, v = d.sem, abs(d.tick)
                    if seen.get(k, 0) >= v:
                        continue
                    seen[k] = v
                    waits.append((k, v))
                inc = n.sem if (n.tick is not None and n.tick > 0) else None
                prog[e].append((waits, n.fn, inc, n.inc))
        for e in self.ENG:
            waits = []
            for k in self.ENG:
                if k != e and self.cnt[k] > self.seen[e].get(k, 0):
                    waits.append((k, self.cnt[k]))
                    self.seen[e][k] = self.cnt[k]
            for k, v in self.dma_cnt.items():
                if self.seen[e].get(k, 0) < v:
                    waits.append((k, v))
                    self.seen[e][k] = v
            if waits:
                prog[e].append((waits, None, None, 0))
        self.phase += 1
        nc = self.nc
        sem = self.sem
        with nc.Block() as block:
            def run(name):
                pr = prog[name]

                def f(eng):
                    for waits, fn, inc, n in pr:
                        for k, v in waits:
                            eng.wait_ge(sem[k], v)
                        if fn is None:
                            continue
                        ins = fn(eng)
                        if inc is not None:
                            ins.then_inc(sem[inc], n)
                return f

            block.tensor(run("pe"))
            block.scalar(run("act"))
            block.vector(run("dve"))
            block.gpsimd(run("pool"))
            block.sync(run("sp"))

import contextlib
import numpy as np

D = 1024
T = 2048
NS = 16
NT = T + NS
DFF = 2816
NPAIR = 22
EPS = 1e-6
SBS = [(0, 512), (512, 512), (1024, 512), (1536, 512), (2048, 16)]
BLOCKS = [[0], [1], [2], [3, 4]]

P_N1, P_N2, P_FN, P_PS, P_FCW, P_FCB, P_DCW, P_ONW, P_AL, P_DT, NPAR = 0, 16, 32, 40, 48, 312, 400, 496, 497, 498, 500
C_ID, C_ML, C_MU, C_CM, C_RT, C_ONE, C_I16, NCON = 0, 128, 256, 384, 512, 576, 704, 960


def build_program(with_dn=True, dn_hook=None):
    nc = bass.Bass("TRN2", target_bir_lowering=False)

    def din(name, shape):
        return nc.dram_tensor(name, list(shape), F32, kind="ExternalInput").ap()

    def dout(name, shape):
        return nc.dram_tensor(name, list(shape), F32, kind="ExternalOutput").ap()

    xin = din("xin", [D, NT])
    wup = din("wup", [2, NPAIR, 128, 2048])
    wdn = din("wdn", [2, 8, 128, NPAIR * 128])
    win = din("win", [33, 128, 1024])
    wout = din("wout", [8, 128, 1024])
    poolw = din("poolw", [128, 2048])
    params = din("params", [128, NPAR])
    consts = din("consts", [128, NCON])
    pstate = din("pstate", [128, 8 * 16 * 15])
    fstate = din("fstate", [2, 128, 44 * 32])
    dcstate = din("dcstate", [128, 24 * 48])
    sstate = din("sstate", [16, 128, 1024])
    pool_raw = din("pool_raw", [16, 15 * 1024])
    f_raw = din("f_raw", [2, 16, 2 * 5632])
    dc_raw = din("dc_raw", [16, 3 * 3072])

    yT = dout("yT", [D, NT])
    pool_new = dout("pool_new", [D, 31])
    pool_old = dout("pool_old", [16, 14 * 1024])
    dnc_new = dout("dnc_new", [3072, 19])
    dnc_old = dout("dnc_old", [16, 2 * 3072])
    ffn_new = dout("ffn_new", [2, 5632, 18])
    ffn_old = dout("ffn_old", [2, 16, 5632])
    ssm_p = dout("ssm_p", [8, 128, 128])
    ssm_s = dout("ssm_s", [16, 8, 128, 128])

    with contextlib.ExitStack() as es:
        S = Sched(nc, es)

        uniq = [0]

        def sbt(stack, name, shape, dt):
            uniq[0] += 1
            return stack.enter_context(nc.sbuf_tensor(f"{name}_{uniq[0]}", list(shape), dt))

        xT = sbt(es, "xT", [128, 8, NT], F32)
        par = sbt(es, "par", [128, NPAR], F32)
        con = sbt(es, "con", [128, NCON], F32)
        idb = sbt(es, "idb", [128, 128], BF16)
        RSP = sbt(es, "RSP", [128, NT], F32)
        SQP = sbt(es, "SQP", [128, 4, 512], BF16)
        rsPB = [Buf(f"rsp{s}") for s in range(5)]
        sqPB = [Buf(f"sqp{i}") for i in range(4)]
        sqc = [0]
        oneb = sbt(es, "oneb", [128, 128], BF16)
        xB = [[Buf(f"x{c}_{s}") for s in range(5)] for c in range(8)]
        nB = [[Buf(f"n{c}_{s}") for s in range(5)] for c in range(8)]
        parB, conB, idbB = Buf("par"), Buf("con"), Buf("idb")
        banks = [es.enter_context(nc.psum_tensor(f"pb{i}", [128, 512], F32)) for i in range(8)]
        bankB = [Buf(f"pb{i}") for i in range(8)]
        bctr = [0]

        def pb():
            i = bctr[0] % 8
            bctr[0] += 1
            return banks[i], bankB[i]

        def _dv(ap):
            return ap.free_size() / 960.0 + 0.15

        def _da(ap):
            return ap.free_size() / 1200.0 + 0.22

        def _dp(ap):
            return ap.free_size() * 2.2 / 1200.0 + 0.3

        def _de(eng, ap):
            return _dv(ap) if eng == "dve" else (_da(ap) if eng == "act" else _dp(ap))

        def mm(out, lhsT, rhs, start, stop, rd, wr, sig=True):
            passes = 4 if lhsT.dtype == F32 else 1
            S.op("pe", lambda e: e.matmul(out, lhsT=lhsT, rhs=rhs, start=start, stop=stop), reads=rd, writes=wr,
                 dur=max(0.11, out.free_size() * passes / 1900.0))

        def tt(eng, out, in0, in1, op, rd, wr):
            S.op(eng, lambda e: e.tensor_tensor(out=out, in0=in0, in1=in1, op=op), reads=rd, writes=wr, dur=_de(eng, out))

        def stt(eng, out, in0, scalar, in1, op0, op1, rd, wr):
            S.op(eng, lambda e: e.scalar_tensor_tensor(out=out, in0=in0, scalar=scalar, in1=in1, op0=op0, op1=op1), reads=rd, writes=wr,
                 dur=_de(eng, out))

        def ts(eng, out, in0, s1, s2, op0, op1, rd, wr):
            if s2 is None:
                S.op(eng, lambda e: e.tensor_scalar(out=out, in0=in0, scalar1=s1, scalar2=None, op0=op0), reads=rd, writes=wr, dur=_de(eng, out))
            else:
                S.op(eng, lambda e: e.tensor_scalar(out=out, in0=in0, scalar1=s1, scalar2=s2, op0=op0, op1=op1), reads=rd, writes=wr,
                     dur=_de(eng, out))

        def act(out, in_, func, rd, wr, **kw):
            S.op("act", lambda e: e.activation(out=out, in_=in_, func=func, **kw), reads=rd, writes=wr, dur=_da(out))

        def cp(eng, out, in_, rd, wr):
            if eng == "act":
                S.op("act", lambda e: e.activation(out=out, in_=in_, func=AF.Copy), reads=rd, writes=wr, dur=_da(out))
            else:
                S.op(eng, lambda e: e.tensor_copy(out=out, in_=in_), reads=rd, writes=wr, dur=_de(eng, out))

        def memset(eng, ap, val, wr):
            S.op(eng, lambda e: e.memset(ap, val), writes=wr, dur=ap.free_size() / 1900.0 + 0.1)

        def dma(q, out, in_, key, rd=(), wr=()):
            S.dma(q, lambda e: e.dma_start(out=out, in_=in_), key, reads=rd, writes=wr, nbytes=max(out.nbytes(), in_.nbytes()))

        def allx(c):
            return xB[c]

        H = dict(nc=nc, S=S, sbt=sbt, xT=xT, par=par, con=con, idb=idb, oneb=oneb, xB=xB, nB=nB,
                 parB=parB, conB=conB, idbB=idbB, pb=pb, mm=mm, tt=tt, stt=stt, ts=ts, act=act, cp=cp,
                 memset=memset, dma=dma, win=win, wout=wout, dcstate=dcstate, sstate=sstate, dc_raw=dc_raw,
                 dnc_new=dnc_new, dnc_old=dnc_old, ssm_p=ssm_p, ssm_s=ssm_s, banks=banks, bankB=bankB)

        dma("sp", par[:, :], params, "ld_par", wr=[parB])
        dma("sp", con[:, :], consts, "ld_con", wr=[conB])
        xin3 = xin.rearrange("(c p) t -> p c t", p=128)
        for s, (o, n) in enumerate(SBS):
            dma("sp", xT[:, :, o:o + n], xin3[:, :, o:o + n], ("ldx", s), wr=[xB[c][s] for c in range(8)])
        cp("dve", idb[:, :], con[:, C_ID:C_ID + 128], [conB], [idbB])
        cp("dve", oneb[:, :], con[:, C_ONE:C_ONE + 128], [conB], [idbB])
        dma("sp", pool_old, pool_raw[:, 1024:15 * 1024], "d2d")
        dma("sp", dnc_old, dc_raw[:, 3072:3 * 3072], "d2d")
        for l in range(2):
            dma("sp", ffn_old[l], f_raw[l][:, 5632:2 * 5632], "d2d")

        def emit_stats(bank_fn, all_act=False, subs=(0, 1, 2, 3, 4)):
            for s in subs:
                o, n = SBS[s]
                bk, bb = bank_fn()
                for c in range(8):
                    slot = sqc[0] % 4
                    sqc[0] += 1
                    sq = SQP[:, slot, 0:n]
                    if c % 2 == 0 or all_act:
                        act(sq, xT[:, c, o:o + n], AF.Square, [xB[c][s]], [sqPB[slot]])
                    else:
                        tt("dve", sq, xT[:, c, o:o + n], xT[:, c, o:o + n], ALU.mult, [xB[c][s]], [sqPB[slot]])
                    mm(bk[:, 0:n], oneb[:, :], sq, c == 0, c == 7, [sqPB[slot], idbB], [bb])
                act(RSP[:, o:o + n], bk[:, 0:n], AF.Ln, [bb], [rsPB[s]], scale=1.0 / D, bias=EPS)
                act(RSP[:, o:o + n], RSP[:, o:o + n], AF.Exp, [rsPB[s]], [rsPB[s]], scale=-0.5)

        def apply_norm(wcol0, XN):
            for s, (o, n) in enumerate(SBS):
                for c in range(8):
                    stt("dve", XN[:, c, o:o + n], xT[:, c, o:o + n], par[:, wcol0 + c:wcol0 + c + 1], RSP[:, o:o + n], ALU.mult, ALU.mult,
                        [xB[c][s], rsPB[s], parB], [nB[c][s]])

        with contextlib.ExitStack() as st:
            XN = sbt(st, "XN", [128, 8, NT], BF16)
            emit_stats(pb, all_act=True)
            RS, rsB = RSP, None
            xnf = [sbt(st, f"xnf{i}", [128, 16 + NT], F32) for i in range(2)]
            xfB = [Buf("xf0"), Buf("xf1")]
            PB = sbt(st, "PB", [128, 16 + T], F32)
            PC = sbt(st, "PC", [128, 16 + T], F32)
            pbB, pcB = Buf("PB"), Buf("PC")
            pst = sbt(st, "pst", [128, 8, 16, 15], F32)
            pstB = Buf("pst")
            pw = sbt(st, "pw", [128, 4, 2, 256], BF16)
            pwB = Buf("pw")
            red = sbt(st, "red", [128, 16], F32)
            redB = Buf("red")
            dma("sp", pst[:, :, :, :].rearrange("p a b c -> p (a b c)"), pstate, "ld_pst", wr=[pstB])
            dma("pool", pw[:, :, :, :].rearrange("p a b c -> p (a b c)"), poolw, "ld_pw", wr=[pwB])
            for i in range(2):
                memset("dve", xnf[i][:, 0:16], 0.0, [xfB[i]])
            for c in range(8):
                g = c // 2
                w = 2 ** (g + 1)
                xf, xb_ = xnf[c % 2], xfB[c % 2]
                stt("dve", xf[:, 16:16 + NT], xT[:, c, :], par[:, P_N1 + c:P_N1 + c + 1], RSP[:, :], ALU.mult, ALU.mult,
                    xB[c] + rsPB + [parB], [xb_])
                dma("sp", pool_new[c * 128:(c + 1) * 128, :], xf[:, 16 + 2033:16 + NT], ("st_pool", c % 2), rd=[xb_])
                E_ = 16 + T
                weng = "dve"
                tt(weng, PB[:, 1:E_], xf[:, 1:E_], xf[:, 0:E_ - 1], ALU.add, [xb_], [pbB])
                s_ap, s_b = PB, pbB
                if g >= 1:
                    tt(weng, PC[:, 3:E_], PB[:, 3:E_], PB[:, 1:E_ - 2], ALU.add, [pbB], [pcB])
                    s_ap, s_b = PC, pcB
                if g >= 2:
                    tt(weng, PB[:, 7:E_], PC[:, 7:E_], PC[:, 3:E_ - 4], ALU.add, [pcB], [pbB])
                    s_ap, s_b = PB, pbB
                if g >= 3:
                    tt(weng, PC[:, 15:E_], PB[:, 15:E_], PB[:, 7:E_ - 8], ALU.add, [pbB], [pcB])
                    s_ap, s_b = PC, pcB
                tt("dve", s_ap[:, 16:32], s_ap[:, 16:32], con[:, C_RT + g * 16:C_RT + g * 16 + 16], ALU.mult, [s_b, conB], [s_b])
                stt("dve", XN[:, c, 0:T], s_ap[:, 16:E_], 1.0 / w, xf[:, 16:E_], ALU.mult, ALU.subtract,
                    [s_b, xb_], nB[c][0:4])
                S.op("dve", lambda e, c=c, w=w: e.tensor_reduce(out=red[:, :], in_=pst[:, c, :, 16 - w:15], axis=AX.X, op=ALU.add),
                     reads=[pstB], writes=[redB])
                tt("dve", red[:, :], red[:, :], xf[:, E_:16 + NT], ALU.add, [redB, xb_], [redB])
                stt("dve", XN[:, c, T:NT], red[:, :], 1.0 / w, xf[:, E_:16 + NT], ALU.mult, ALU.subtract,
                    [redB, xb_], [nB[c][4]])
            for g in range(4):
                for dc in range(2):
                    co = 2 * g + dc
                    for s, (o, n) in enumerate(SBS):
                        bk, bb = pb()
                        for cc in range(2):
                            mm(bk[:, 0:n], pw[:, g, cc, dc * 128:(dc + 1) * 128], XN[:, 2 * g + cc, o:o + n], cc == 0, cc == 1,
                               [pwB, nB[2 * g + cc][s]], [bb])
                        stt("dve", xT[:, co, o:o + n], bk[:, 0:n], par[:, P_PS + co:P_PS + co + 1], xT[:, co, o:o + n],
                            ALU.mult, ALU.add, [bb, parB, xB[co][s]], [xB[co][s]])
            emit_stats(pb, all_act=True)
            S.barrier()
            S.emit(lat=1.0)

        FBLK = [[0, 1], [2, 3, 4]]

        def ffn(l):
            with contextlib.ExitStack() as st:
                W = 1024
                XN = sbt(st, "XNf", [128, 8, W + 16], BF16)
                xnl = [[Buf(f"xnl{c}_{j}") for j in range(3)] for c in range(8)]
                if l == 1 and with_dn:
                    emit_stats(pb)
                ACTB = sbt(st, "ACTB", [128, NPAIR, W + 16], BF16)
                aB = [Buf(f"a{i}") for i in range(NPAIR)]
                NH = 4
                Hb = [sbt(st, f"H{i}", [128, 2 + W], F32) for i in range(NH)]
                hB = [Buf(f"h{i}") for i in range(NH)]
                TT = [sbt(st, f"TT{i}", [128, W], F32) for i in range(2)]
                ttB = [Buf("tt0"), Buf("tt1")]
                SG = sbt(st, "SG", [128, W + 16], BF16)
                sgB = Buf("sg")
                sm1 = [sbt(st, f"sm1_{i}", [128, 16], F32) for i in range(2)]
                s1B = [Buf("s10"), Buf("s11")]
                newh = sbt(st, "newh", [128, 44, 18], F32)
                nhB = [Buf(f"nh{i}") for i in range(44)]
                fst = sbt(st, "fst", [128, 44, 16, 2], F32)
                fstB = Buf("fst")
                NWU = 3
                wu = [sbt(st, f"wu{i}", [128, 8, 2, 128], BF16) for i in range(NWU)]
                wuB = [Buf(f"wu{i}") for i in range(NWU)]
                wd = [sbt(st, f"wd{i}", [128, NPAIR, 128], BF16) for i in range(2)]
                wdB = [Buf("wd0"), Buf("wd1")]
                dma("sp", fst[:, :, :, :].rearrange("p a b c -> p (a b c)"), fstate[l], "ld_fst", wr=[fstB])
                memset("dve", newh[:, :, 0:2], 0.0, nhB)
                cnt = {"wu": 0, "wd": 0, "u": 0}

                def pbu():
                    i = cnt["u"] % 8
                    cnt["u"] += 1
                    return banks[i], bankB[i]

                def load_wu(i):
                    k = cnt["wu"] % NWU
                    cnt["wu"] += 1
                    dma("pool", wu[k][:, :, :, :].rearrange("p a b c -> p (a b c)"), wup[l, i], ("wu", k), wr=[wuB[k]])
                    return k

                for bi, blk in enumerate(FBLK):
                    b0 = SBS[blk[0]][0]
                    has_s = 4 in blk
                    loc = {s: (SBS[s][0] - b0 if s < 4 else W) for s in blk}
                    jdx = {s: j for j, s in enumerate(blk)}
                    for s in blk:
                        o, n = SBS[s]
                        for c in range(8):
                            stt("dve", XN[:, c, loc[s]:loc[s] + n], xT[:, c, o:o + n], par[:, P_N2 + 8 * l + c:P_N2 + 8 * l + c + 1],
                                RSP[:, o:o + n], ALU.mult, ALU.mult, [xB[c][s], rsPB[s], parB], [xnl[c][jdx[s]]])
                    slots = {}
                    PRE = 2
                    for i in range(min(PRE, NPAIR)):
                        slots[i] = load_wu(i)

                    def stage_a(i):
                        if i + PRE < NPAIR:
                            slots[i + PRE] = load_wu(i + PRE)
                        k = slots[i]
                        for gv in range(2):
                            ch = gv * NPAIR + i
                            hi_ = (i % 2) * 2 + gv
                            Ht, hb_ = Hb[hi_], hB[hi_]
                            cp("act", Ht[:, 0:2], newh[:, ch, 0:2], [nhB[ch]], [hb_])
                            for s in blk:
                                o, n = SBS[s]
                                bk, bb = pbu()
                                for kc in range(8):
                                    mm(bk[:, 0:n], wu[k][:, kc, gv, :], XN[:, kc, loc[s]:loc[s] + n], kc == 0, kc == 7,
                                       [wuB[k], xnl[kc][jdx[s]]], [bb], sig=(kc == 7))
                                if s < 4:
                                    cp("act", Ht[:, 2 + o - b0:2 + o - b0 + n], bk[:, 0:n], [bb], [hb_])
                                else:
                                    cp("act", newh[:, ch, 2:18], bk[:, 0:16], [bb], [nhB[ch]])
                            cp("act", newh[:, ch, 0:2], Ht[:, W:W + 2], [hb_], [nhB[ch]])

                    def stage_b(i):
                        for gv in range(2):
                            ch = gv * NPAIR + i
                            hi_ = (i % 2) * 2 + gv
                            Ht, hb_ = Hb[hi_], hB[hi_]
                            w0 = par[:, P_FCW + (l * 3 + 0) * 44 + ch:P_FCW + (l * 3 + 0) * 44 + ch + 1]
                            w1 = par[:, P_FCW + (l * 3 + 1) * 44 + ch:P_FCW + (l * 3 + 1) * 44 + ch + 1]
                            w2 = par[:, P_FCW + (l * 3 + 2) * 44 + ch:P_FCW + (l * 3 + 2) * 44 + ch + 1]
                            bb_ = par[:, P_FCB + l * 44 + ch:P_FCB + l * 44 + ch + 1]
                            t_, tb_ = TT[gv], ttB[gv]
                            ts("pool", t_[:, 0:W], Ht[:, 2:2 + W], w2, bb_, ALU.mult, ALU.add, [hb_, parB], [tb_])
                            stt("dve", t_[:, 0:W], Ht[:, 1:1 + W], w1, t_[:, 0:W], ALU.mult, ALU.add, [hb_, parB, tb_], [tb_])
                            stt("dve", t_[:, 0:W], Ht[:, 0:W], w0, t_[:, 0:W], ALU.mult, ALU.add, [hb_, parB, tb_], [tb_])
                            if has_s:
                                a1, a1b = sm1[gv], s1B[gv]
                                ts("dve", a1[:, :], newh[:, ch, 2:18], w2, bb_, ALU.mult, ALU.add, [nhB[ch], parB], [a1b])
                                stt("dve", a1[:, :], fst[:, ch, :, 1], w1, a1[:, :], ALU.mult, ALU.add, [fstB, parB, a1b], [a1b])
                                stt("dve", a1[:, :], fst[:, ch, :, 0], w0, a1[:, :], ALU.mult, ALU.add, [fstB, parB, a1b], [a1b])
                        act(SG[:, 0:W], TT[0][:, 0:W], AF.Silu, [ttB[0]], [sgB])
                        if has_s:
                            act(SG[:, W:W + 16], sm1[0][:, :], AF.Silu, [s1B[0]], [sgB])
                        tt("dve", ACTB[:, i, 0:W], SG[:, 0:W], TT[1][:, 0:W], ALU.mult, [sgB, ttB[1]], [aB[i]])
                        if has_s:
                            tt("dve", ACTB[:, i, W:W + 16], SG[:, W:W + 16], sm1[1][:, :], ALU.mult, [sgB, s1B[1]], [aB[i]])

                    for i in range(NPAIR + 1):
                        if i < NPAIR:
                            stage_a(i)
                        if i >= 1:
                            stage_b(i - 1)
                    for dc in range(8):
                        k = cnt["wd"] % 2
                        cnt["wd"] += 1
                        dma("pool", wd[k][:, :, :].rearrange("p a b -> p (a b)"), wdn[l, dc], ("wd", k), wr=[wdB[k]])
                        for s in blk:
                            o, n = SBS[s]
                            lo = o - b0 if s < 4 else W
                            bk, bb = pbu()
                            for i in range(NPAIR):
                                mm(bk[:, 0:n], wd[k][:, i, :], ACTB[:, i, lo:lo + n], i == 0, i == NPAIR - 1,
                                   [wdB[k], aB[i]], [bb], sig=(i == NPAIR - 1))
                            tt("dve", xT[:, dc, o:o + n], bk[:, 0:n], xT[:, dc, o:o + n], ALU.add, [bb, xB[dc][s]], [xB[dc][s]])
                dma("sp", ffn_new[l].rearrange("(c p) k -> p c k", p=128), newh[:, :, :], "st_ffn", rd=nhB)
                emit_stats(pbu)
                S.barrier()
                S.emit(lat=1.0)

        ffn(0)
        if with_dn:
            H["RSP"] = RSP
            H["SQP"] = SQP
            H["emit_stats"] = emit_stats
            dn_hook(H)
        ffn(1)

        with contextlib.ExitStack() as st:
            YO = [sbt(st, f"YO{i}", [128, NT], F32) for i in range(2)]
            yB = [Buf("y0"), Buf("y1")]
            for c in range(8):
                stt("dve", YO[c % 2][:, :], xT[:, c, :], par[:, P_FN + c:P_FN + c + 1], RSP[:, :], ALU.mult, ALU.mult,
                    xB[c] + rsPB + [parB], [yB[c % 2]])
                dma("sp", yT[c * 128:(c + 1) * 128, :], YO[c % 2][:, :], ("st_y", c % 2), rd=[yB[c % 2]])
            S.barrier()
            S.finish("sp")
            S.emit()
    return nc

import contextlib


def dn_hook(H):
    nc, S, sbt = H["nc"], H["S"], H["sbt"]
    xT, par, con, idb, oneb = H["xT"], H["par"], H["con"], H["idb"], H["oneb"]
    xB, parB, conB, idbB = H["xB"], H["parB"], H["conB"], H["idbB"]
    mm, tt, stt, ts, act, cp, memset, dma = H["mm"], H["tt"], H["stt"], H["ts"], H["act"], H["cp"], H["memset"], H["dma"]
    banks, bankB = H["banks"], H["bankB"]
    win, wout, dcstate, sstate = H["win"], H["wout"], H["dcstate"], H["sstate"]
    dnc_new, ssm_p, ssm_s = H["dnc_new"], H["ssm_p"], H["ssm_s"]
    pctr = {"p": 0, "t": 0, "s": 0}

    def pbp():
        i = pctr["p"] % 3
        pctr["p"] += 1
        return banks[i], bankB[i]

    def pbt():
        i = 3 + pctr["t"] % 5
        pctr["t"] += 1
        return banks[i], bankB[i]

    def pbs():
        i = pctr["s"] % 8
        pctr["s"] += 1
        return banks[i], bankB[i]

    idf = con[:, C_ID:C_ID + 128]
    NB = 8
    NLEV = 5
    DB = 256
    with contextlib.ExitStack() as st:
        RSd = H["RSP"]; rsdB = Buf("rsd")
        xD = [[Buf(f"xd{c}_{b}") for b in range(NB + 1)] for c in range(8)]
        newq = sbt(st, "newq", [128, 24, 19], F32); nqB = [Buf(f"nq{i}") for i in range(24)]
        dst = sbt(st, "dst", [128, 24, 16, 3], F32); dstB = Buf("dst")
        QK = [sbt(st, f"QK{p}", [128, 16, DB + 16], BF16) for p in range(2)]
        VT = [sbt(st, f"VT{p}", [128, 8, DB + 16], BF16) for p in range(2)]
        ZS = [sbt(st, f"ZS{p}", [128, 8, DB + 16], BF16) for p in range(2)]
        OG = ZS
        qkB = [[Buf(f"qk{p}_{h}") for h in range(16)] for p in range(2)]
        vB = [[Buf(f"v{p}_{h}") for h in range(8)] for p in range(2)]
        zB = [[Buf(f"z{p}_{h}") for h in range(8)] for p in range(2)]
        ogB = zB
        GA = [sbt(st, f"GA{p}", [8, DB + 16], F32) for p in range(2)]
        GB = [sbt(st, f"GB{p}", [8, DB + 16], F32) for p in range(2)]
        GC = [sbt(st, f"GC{p}", [8, DB], F32) for p in range(2)]
        DK = [sbt(st, f"DK{p}", [8, DB], F32) for p in range(2)]
        gaB = [Buf("ga0"), Buf("ga1")]; gbB = [Buf("gb0"), Buf("gb1")]
        gcB = [Buf("gc0"), Buf("gc1")]; dkB = [Buf("dk0"), Buf("dk1")]
        SC = [sbt(st, f"SC{p}", [128, 2, 24], F32) for p in range(2)]
        EX = [sbt(st, f"EX{p}", [128, 2, 24], F32) for p in range(2)]
        BK = [sbt(st, f"BK{p}", [128, 2, 8], F32) for p in range(2)]
        scB = [Buf("sc0"), Buf("sc1")]; exB = [Buf("ex0"), Buf("ex1")]; bkB = [Buf("bk0"), Buf("bk1")]
        GM = sbt(st, "GM", [8, 128], F32); gmB = Buf("gm")
        nA = sbt(st, "nA", [8, 1], F32); naB = Buf("na")
        S32 = sbt(st, "S32", [128, 8, 128], F32); SBF = sbt(st, "SBF", [128, 8, 128], BF16)
        s32B = [Buf(f"s32_{h}") for h in range(8)]; sbfB = Buf("sbf")
        Otm = sbt(st, "Otm", [128, 8, 128], F32)
        otB = Buf("otm")
        SS = sbt(st, "SS", [128, 8], F32); ssB = Buf("ss")
        EGL = sbt(st, "EGL", [128, 8, 2], F32); eglB = Buf("egl")

        dma("sp", dst[:, :, :, :].rearrange("p a b c -> p (a b c)"), dcstate, "ld_dst", wr=[dstB])
        memset("dve", newq[:, :, 0:3], 0.0, nqB)
        memset("dve", S32[:, :, :], 0.0, s32B)
        memset("dve", SBF[:, :, :], 0.0, [sbfB])
        act(nA[:, :], par[0:8, P_AL:P_AL + 1], AF.Exp, [parB], [naB])
        ts("dve", nA[:, :], nA[:, :], -1.0, None, ALU.mult, None, [naB], [naB])

        wo = [sbt(st, f"wo{i}", [128, 8, 128], BF16) for i in range(2)]
        woB = [Buf("wo0"), Buf("wo1")]
        sa = contextlib.ExitStack()
        NWI = 4
        wi = [sbt(sa, f"wi{i}", [128, 8, 128], BF16) for i in range(NWI)]
        wiB = [Buf(f"wi{i}") for i in range(NWI)]
        Hq = [sbt(sa, f"Hq{i}", [128, 3 + DB], F32) for i in range(2)]
        hqB = [Buf("hq0"), Buf("hq1")]
        TA = sbt(sa, "TA", [128, DB + 16], F32); TB = sbt(sa, "TB", [128, DB + 16], F32)
        taB, tbB = Buf("ta"), Buf("tb")
        XNb1 = sbt(sa, "XNb", [128, 8, DB + 16], BF16)
        XNb = [XNb1, XNb1]
        xnB1 = Buf("xnb")
        xnB = [xnB1, xnB1]
        SQ8 = H["SQP"][:, :, 0:DB]; sq8B = Buf("sq8")
        RQ8 = sbt(sa, "RQ8", [128, 4, DB], F32); rq8B = Buf("rq8")
        SQs = sbt(sa, "SQs", [128, 16, 16], BF16)
        RQs = sbt(sa, "RQs", [128, 16, 16], F32); rqsB = Buf("rqs")
        QD = sbt(sa, "QD", [128, 8, 128], BF16); qdB = Buf("qd")
        KDEC = sbt(sa, "KDEC", [128, 8, 128], BF16); kdB = Buf("kdec")
        QKT = sbt(sa, "QKT", [128, 8, 128], BF16); qktB = Buf("qkt")
        WT = sbt(sa, "WT", [128, 8, 128], BF16); wtB = Buf("wt")
        U = sbt(sa, "U", [128, 8, 128], BF16); uB = Buf("u")
        ON, onB = U, uB
        USOL = sbt(sa, "USOL", [128, 8, 128], F32); usB = Buf("usol")
        ET = [sbt(sa, f"ET{i}", [128, 4, 128], F32) for i in range(2)]; etB = [Buf("et0"), Buf("et1")]
        RA = [sbt(sa, f"Ra{i}", [128, 4, 256], BF16) for i in range(2)]
        raB = [Buf("ra0"), Buf("ra1")]
        AA = [[sbt(sa, f"A{i}_{j}", [128, 4, 128], BF16) for j in range(4)] for i in range(2)]
        AAB = [[Buf(f"A{i}_{j}") for j in range(4)] for i in range(2)]
        GCB = [sbt(sa, f"GCB{i}", [128, 4, 128], F32) for i in range(2)]; gcbB = [Buf("gcb0"), Buf("gcb1")]
        RBt = [GCB[i][:, :, :].bitcast(BF16) for i in range(2)]
        rbB = gcbB
        DF = [sbt(sa, f"DF{i}", [128, 4, 128], F32) for i in range(2)]; dfB = [Buf("df0"), Buf("df1")]
        scr, scrB = USOL, usB

        wic = [0]
        woc = [0]
        hqc = [0]

        def proj_gen(bi):
            p = bi % 2
            last = bi == NB - 1
            c0 = bi * DB
            WB = DB + 16 if last else DB
            for kc in range(8):
                stt("dve", XNb[p][:, kc, 0:DB], xT[:, kc, c0:c0 + DB], par[:, P_N1 + 8 + kc:P_N1 + 9 + kc], RSd[:, c0:c0 + DB],
                    ALU.mult, ALU.mult, [xD[kc][bi], rsdB, parB], [xnB[p]])
                if last:
                    stt("dve", XNb[p][:, kc, DB:DB + 16], xT[:, kc, T:NT], par[:, P_N1 + 8 + kc:P_N1 + 9 + kc], RSd[:, T:NT],
                        ALU.mult, ALU.mult, [xD[kc][NB], rsdB, parB], [xnB[p]])
            yield
            for oc in range(33):
                k = wic[0] % NWI
                wic[0] += 1
                dma("pool", wi[k][:, :, :].rearrange("p a b -> p (a b)"), win[oc], ("wi", k), wr=[wiB[k]])
                rdw = [wiB[k], xnB[p]]
                if oc < 24:
                    if oc < 16:
                        dest, dB, hsel = QK[p], qkB[p][oc], oc
                    else:
                        dest, dB, hsel = VT[p], vB[p][oc - 16], oc - 16
                    Hq_, hqb = Hq[hqc[0] % 2], hqB[hqc[0] % 2]
                    hqc[0] += 1
                    cp("act", Hq_[:, 0:3], newq[:, oc, 0:3], [nqB[oc]], [hqb])
                    bk, bb = pbp()
                    for kc in range(8):
                        mm(bk[:, 0:WB], wi[k][:, kc, :], XNb[p][:, kc, 0:WB], kc == 0, kc == 7, rdw, [bb], sig=(kc == 7))
                    cp("act", Hq_[:, 3:3 + DB], bk[:, 0:DB], [bb], [hqb])
                    if last:
                        cp("act", newq[:, oc, 3:19], bk[:, DB:DB + 16], [bb], [nqB[oc]])
                    cp("act", newq[:, oc, 0:3], Hq_[:, DB:DB + 3], [hqb], [nqB[oc]])
                    w = [par[:, P_DCW + j * 24 + oc:P_DCW + j * 24 + oc + 1] for j in range(4)]
                    ts("dve", TA[:, 0:DB], Hq_[:, 3:3 + DB], w[3], None, ALU.mult, None, [hqb, parB], [taB])
                    stt("dve", TB[:, 0:DB], Hq_[:, 2:2 + DB], w[2], TA[:, 0:DB], ALU.mult, ALU.add, [hqb, parB, taB], [tbB])
                    stt("dve", TA[:, 0:DB], Hq_[:, 1:1 + DB], w[1], TB[:, 0:DB], ALU.mult, ALU.add, [hqb, parB, tbB], [taB])
                    stt("dve", TB[:, 0:DB], Hq_[:, 0:DB], w[0], TA[:, 0:DB], ALU.mult, ALU.add, [hqb, parB, taB], [tbB])
                    if last:
                        ts("dve", TA[:, DB:DB + 16], newq[:, oc, 3:19], w[3], None, ALU.mult, None, [nqB[oc], parB], [taB])
                        stt("dve", TB[:, DB:DB + 16], dst[:, oc, :, 2], w[2], TA[:, DB:DB + 16], ALU.mult, ALU.add, [dstB, parB, taB], [tbB])
                        stt("dve", TA[:, DB:DB + 16], dst[:, oc, :, 1], w[1], TB[:, DB:DB + 16], ALU.mult, ALU.add, [dstB, parB, tbB], [taB])
                        stt("dve", TB[:, DB:DB + 16], dst[:, oc, :, 0], w[0], TA[:, DB:DB + 16], ALU.mult, ALU.add, [dstB, parB, taB], [tbB])
                    act(dest[:, hsel, 0:WB], TB[:, 0:WB], AF.Silu, [tbB], [dB])
                elif oc < 32:
                    h = oc - 24
                    bk, bb = pbp()
                    for kc in range(8):
                        mm(bk[:, 0:WB], wi[k][:, kc, :], XNb[p][:, kc, 0:WB], kc == 0, kc == 7, rdw, [bb], sig=(kc == 7))
                    act(ZS[p][:, h, 0:WB], bk[:, 0:WB], AF.Silu, [bb], [zB[p][h]])
                else:
                    bka, bba = pbp()
                    for kc in range(8):
                        mm(bka[0:8, 0:WB], wi[k][:, kc, 0:8], XNb[p][:, kc, 0:WB], kc == 0, kc == 7, rdw, [bba], sig=(kc == 7))
                    bkb, bbb = pbp()
                    for kc in range(8):
                        mm(bkb[0:8, 0:WB], wi[k][:, kc, 8:16], XNb[p][:, kc, 0:WB], kc == 0, kc == 7, rdw, [bbb], sig=(kc == 7))
                    act(GA[p][:, 0:WB], bka[0:8, 0:WB], AF.Exp, [bba, parB], [gaB[p]], bias=par[0:8, P_DT:P_DT + 1])
                    act(GA[p][:, 0:WB], GA[p][:, 0:WB], AF.Ln, [gaB[p]], [gaB[p]], bias=1.0)
                    ts("dve", GA[p][:, 0:WB], GA[p][:, 0:WB], nA[:, 0:1], None, ALU.mult, None, [gaB[p], naB], [gaB[p]])
                    act(GB[p][:, 0:WB], bkb[0:8, 0:WB], AF.Exp, [bbb], [gbB[p]], scale=-1.0)
                    ts("dve", GB[p][:, 0:WB], GB[p][:, 0:WB], 1.0, None, ALU.add, None, [gbB[p]], [gbB[p]])
                    S.op("dve", lambda e, p=p, WB=WB: e.reciprocal(out=GB[p][:, 0:WB], in_=GB[p][:, 0:WB]), reads=[gbB[p]], writes=[gbB[p]])
                yield
            for g4 in range(4):
                hb_ = qkB[p][g4 * 4:g4 * 4 + 4]
                s1, s2 = (128.0, 128.0 * EPS) if g4 < 2 else (1.0, EPS)
                act(SQ8[:, :, :], QK[p][:, g4 * 4:g4 * 4 + 4, 0:DB], AF.Square, hb_, [sq8B])
                for j2 in range(2):
                    bk, bb = pbp()
                    for j in range(2):
                        mm(bk[:, j * DB:(j + 1) * DB], oneb[:, :], SQ8[:, j2 * 2 + j, :], True, True, [sq8B, idbB], [bb])
                    act(RQ8[:, j2 * 2:j2 * 2 + 2, :], bk[:, :].rearrange("p (a b) -> p a b", a=2), AF.Ln, [bb], [rq8B], scale=s1, bias=s2)
                act(RQ8[:, :, :], RQ8[:, :, :], AF.Exp, [rq8B], [rq8B], scale=-0.5)
                tt("dve", QK[p][:, g4 * 4:g4 * 4 + 4, 0:DB], QK[p][:, g4 * 4:g4 * 4 + 4, 0:DB], RQ8[:, :, :], ALU.mult, hb_ + [rq8B], hb_)
                yield
            if last:
                act(SQs[:, :, :], QK[p][:, :, DB:DB + 16], AF.Square, qkB[p], [rqsB])
                bk, bb = pbp()
                for hd in range(16):
                    mm(bk[:, hd * 16:(hd + 1) * 16], oneb[:, :], SQs[:, hd, :], True, True, [rqsB, idbB], [bb])
                act(RQs[:, 0:8, :], bk[:, 0:128].rearrange("p (a b) -> p a b", a=8), AF.Ln, [bb], [rqsB], scale=128.0, bias=128.0 * EPS)
                act(RQs[:, 8:16, :], bk[:, 128:256].rearrange("p (a b) -> p a b", a=8), AF.Ln, [bb], [rqsB], scale=1.0, bias=EPS)
                act(RQs[:, :, :], RQs[:, :, :], AF.Exp, [rqsB], [rqsB], scale=-0.5)
                tt("dve", QK[p][:, :, DB:DB + 16], QK[p][:, :, DB:DB + 16], RQs[:, :, :], ALU.mult, qkB[p] + [rqsB], qkB[p])
            for ti in range(2):
                tc = ti * 128
                S.op("dve", lambda e, tc=tc: e.tensor_tensor_scan(out=GC[p][:, tc:tc + 128], data0=con[0:8, C_CM:C_CM + 128], data1=GA[p][:, tc:tc + 128],
                                                                  initial=0.0, op0=ALU.mult, op1=ALU.add), reads=[gaB[p], conB], writes=[gcB[p]])
            for ck in range(4):
                stt("dve", DK[p][:, ck * 64:ck * 64 + 64], GC[p][:, ck * 64:ck * 64 + 64], -1.0,
                    GC[p][:, ck * 64 + 63:ck * 64 + 64].to_broadcast([8, 64]), ALU.mult, ALU.add, [gcB[p]], [dkB[p]])
            for ti in range(2):
                tc = ti * 128
                bk, bb = pbp()
                mm(bk[:, 0:8], GC[p][:, tc:tc + 128], con[0:8, C_ID:C_ID + 8], True, True, [gcB[p], conB], [bb])
                mm(bk[:, 8:16], GB[p][:, tc:tc + 128], con[0:8, C_ID:C_ID + 8], True, True, [gbB[p], conB], [bb])
                mm(bk[:, 16:24], DK[p][:, tc:tc + 128], con[0:8, C_ID:C_ID + 8], True, True, [dkB[p], conB], [bb])
                cp("dve", SC[p][:, ti, :], bk[:, 0:24], [bb], [scB[p]])
            act(EX[p][:, :, :], SC[p][:, :, :], AF.Exp, [scB[p]], [exB[p]])
            tt("dve", BK[p][:, :, :], SC[p][:, :, 8:16], EX[p][:, :, 0:8], ALU.mult, [scB[p], exB[p]], [bkB[p]])
            yield

        def b4(ap):
            return ap.unsqueeze(2).to_broadcast([128, 4, 128])

        def tile_gen(bi, ti):
            p = bi % 2
            tc = ti * 128
            QT = QK[p][:, 0:8, tc:tc + 128]
            KT = QK[p][:, 8:16, tc:tc + 128]
            VTt = VT[p][:, :, tc:tc + 128]
            ZSt = ZS[p][:, :, tc:tc + 128]
            GCt = GC[p][:, tc:tc + 128]
            SCt = SC[p][:, ti, :]
            EXt = EX[p][:, ti, :]
            BKt = BK[p][:, ti, :]
            qB_ = qkB[p][0:8]
            kB_ = qkB[p][8:16]
            for hb in range(2):
                hs = [hb * 4 + j for j in range(4)]
                bkg, bbg = pbt()
                for j, h in enumerate(hs):
                    ts("dve", GM[:, :], GCt, con[0:8, C_ID + h:C_ID + h + 1], None, ALU.mult, None, [gcB[p], conB], [gmB])
                    mm(bkg[:, j * 128:(j + 1) * 128], con[0:8, C_ONE:C_ONE + 128], GM[:, :], True, True, [gmB, conB], [bbg])
                cp("act", GCB[hb][:, :, :].rearrange("p a b -> p (a b)"), bkg[:, :], [bbg], [gcbB[hb]])
                act(EGL[:, hb * 4:hb * 4 + 4, 0], GCB[hb][:, :, 63], AF.Exp, [gcbB[hb]], [eglB])
                act(EGL[:, hb * 4:hb * 4 + 4, 1], GCB[hb][:, :, 127], AF.Exp, [gcbB[hb]], [eglB])
                act(DF[hb][:, :, :], GCB[hb][:, :, :], AF.Exp, [gcbB[hb]], [dfB[hb]])
                tt("dve", QD[:, hb * 4:hb * 4 + 4, :], QT[:, hb * 4:hb * 4 + 4, :], DF[hb][:, :, :], ALU.mult, [dfB[hb]] + [qB_[h] for h in hs], [qdB])
            yield
            for hb in range(2):
                hs = [hb * 4 + j for j in range(4)]
                bkk, bbk = pbt()
                bkq, bbq = pbt()
                bkt, bbt = pbt()
                for j, h in enumerate(hs):
                    sl = slice(j * 128, (j + 1) * 128)
                    mm(bkk[:, sl], KT[:, h, :], KT[:, h, :], True, True, [kB_[h]], [bbk])
                    mm(bkq[:, sl], KT[:, h, :], QT[:, h, :], True, True, [kB_[h], qB_[h]], [bbq])
                    mm(bkt[:, sl], KT[:, h, :], idb[:, :], True, True, [kB_[h], idbB], [bbt])
                k3 = bkt[:, :].rearrange("p (a b) -> p a b", a=4)
                tt("dve", KDEC[:, hb * 4:hb * 4 + 4, :], k3, b4(EXt[:, 16 + hb * 4:20 + hb * 4]), ALU.mult, [bbt, exB[p]], [kdB])
                tt("dve", RA[hb][:, :, 128:256], k3, b4(BKt[:, hb * 4:hb * 4 + 4]), ALU.mult, [bbt, bkB[p]], [raB[hb]])
                tt("dve", DF[hb][:, :, :], GCB[hb][:, :, :], b4(SCt[:, hb * 4:hb * 4 + 4]), ALU.subtract, [gcbB[hb], scB[p], dfB[hb]], [dfB[hb]])
                mL = con[:, C_ML:C_ML + 128].unsqueeze(1).to_broadcast([128, 4, 128])
                mU = con[:, C_MU:C_MU + 128].unsqueeze(1).to_broadcast([128, 4, 128])
                tt("dve", ET[hb][:, :, :], DF[hb][:, :, :], mU, ALU.add, [dfB[hb], conB], [etB[hb]])
                tt("dve", DF[hb][:, :, :], DF[hb][:, :, :], mL, ALU.add, [dfB[hb], conB], [dfB[hb]])
                act(DF[hb][:, :, :], DF[hb][:, :, :], AF.Exp, [dfB[hb]], [dfB[hb]], scale=-1.0)
                act(ET[hb][:, :, :], ET[hb][:, :, :], AF.Exp, [etB[hb]], [etB[hb]])
                tt("dve", DF[hb][:, :, :], DF[hb][:, :, :], b4(SCt[:, 8 + hb * 4:12 + hb * 4]), ALU.mult, [dfB[hb], scB[p]], [dfB[hb]])
                A = AA[hb]
                AB = AAB[hb]
                tt("dve", A[0][:, :, :], bkk[:, :].rearrange("p (a b) -> p a b", a=4), DF[hb][:, :, :], ALU.mult, [bbk, dfB[hb]], [AB[0]])
                tt("dve", QKT[:, hb * 4:hb * 4 + 4, :], bkq[:, :].rearrange("p (a b) -> p a b", a=4), ET[hb][:, :, :], ALU.mult, [bbq, etB[hb]], [qktB])
                bvt, bbv = pbt()
                bkn, bbn = pbt()
                for j, h in enumerate(hs):
                    sl = slice(j * 128, (j + 1) * 128)
                    mm(bvt[:, sl], VTt[:, h, :], idb[:, :], True, True, [vB[p][h], idbB], [bbv])
                for j in range(4):
                    mm(bkn[:, j * 128:(j + 1) * 128], A[0][:, j, :], idb[:, :], True, True, [AB[0], idbB], [bbn])
                v3 = bvt[:, :].rearrange("p (a b) -> p a b", a=4)
                tt("dve", RA[hb][:, :, 0:128], v3, b4(SCt[:, 8 + hb * 4:12 + hb * 4]), ALU.mult, [bbv, scB[p]], [raB[hb]])
                cp("act", A[1][:, :, :].rearrange("p a b -> p (a b)"), bkn[:, :], [bbn], [AB[1]])
                yield
            Rc = [RA[0], RA[1]]; Rn = [RBt[0], RBt[1]]
            rcB = [raB[0], raB[1]]; rnB = [rbB[0], rbB[1]]
            idx = [[0, 1, 2, 3], [0, 1, 2, 3]]
            for lev in range(NLEV):
                for hb in range(2):
                    A, AB = AA[hb], AAB[hb]
                    ak, akT, an, anT = idx[hb]
                    pbk = [pbt(), pbt()]
                    for j in range(4):
                        bk_, bb_ = pbk[j // 2]
                        mm(bk_[:, (j % 2) * 256:(j % 2) * 256 + 256], A[akT][:, j, :], Rc[hb][:, j, :], True, True, [AB[akT], rcB[hb]], [bb_])
                    for q2 in range(2):
                        bk_, bb_ = pbk[q2]
                        tt("dve", Rn[hb][:, 2 * q2:2 * q2 + 2, :], Rc[hb][:, 2 * q2:2 * q2 + 2, :], bk_[:, :].rearrange("p (a b) -> p a b", a=2),
                           ALU.subtract if lev == 0 else ALU.add, [rcB[hb], bb_], [rnB[hb]])
                    Rc[hb], Rn[hb], rcB[hb], rnB[hb] = Rn[hb], Rc[hb], rnB[hb], rcB[hb]
                    if lev < NLEV - 1:
                        b1, bb1 = pbt()
                        for j in range(4):
                            sl = slice(j * 128, (j + 1) * 128)
                            mm(b1[:, sl], A[akT][:, j, :], A[ak][:, j, :], True, True, [AB[ak], AB[akT]], [bb1])
                        cp("act", A[an][:, :, :].rearrange("p a b -> p (a b)"), b1[:, :], [bb1], [AB[an]])
                        b2, bb2 = pbt()
                        for j in range(4):
                            sl = slice(j * 128, (j + 1) * 128)
                            mm(b2[:, sl], A[ak][:, j, :], A[akT][:, j, :], True, True, [AB[ak], AB[akT]], [bb2])
                        cp("act", A[anT][:, :, :].rearrange("p a b -> p (a b)"), b2[:, :], [bb2], [AB[anT]])
                        idx[hb] = [an, anT, ak, akT]
                    yield
            for hb in range(2):
                cp("dve", USOL[:, hb * 4:hb * 4 + 4, :], Rc[hb][:, :, 0:128], [rcB[hb]], [usB])
                bkw, bbw = pbt()
                for j in range(4):
                    mm(bkw[:, j * 128:(j + 1) * 128], Rc[hb][:, j, 128:256], idb[:, :], True, True, [rcB[hb], idbB], [bbw])
                cp("act", WT[:, hb * 4:hb * 4 + 4, :].rearrange("p a b -> p (a b)"), bkw[:, :], [bbw], [wtB])
            yield
            for ck in range(2):
                r0 = ck * 64
                rs_ = slice(r0, r0 + 64)
                ws = [pbt(), pbt()]
                for h in range(8):
                    bk_, bb_ = ws[h // 4]
                    mm(bk_[:, (h % 4) * 128:(h % 4) * 128 + 128], WT[:, h, :], SBF[:, h, :], True, True, [wtB, sbfB], [bb_])
                for q2 in range(2):
                    bk_, bb_ = ws[q2]
                    tt("dve", U[rs_, q2 * 4:q2 * 4 + 4, :], USOL[rs_, q2 * 4:q2 * 4 + 4, :], bk_[rs_, :].rearrange("p (a b) -> p a b", a=4),
                       ALU.subtract, [usB, bb_], [uB])
                po = [pbt(), pbt()]
                for h in range(8):
                    bk_, bb_ = po[h // 4]
                    sl = slice((h % 4) * 128, (h % 4) * 128 + 128)
                    mm(bk_[:, sl], QD[:, h, :], SBF[:, h, :], True, False, [qdB, sbfB], [bb_])
                    mm(bk_[:, sl], QKT[rs_, h, :], U[rs_, h, :], False, True, [qktB, uB], [bb_])
                for q2 in range(2):
                    bk_, bb_ = po[q2]
                    cp("act", Otm[rs_, q2 * 4:q2 * 4 + 4, :].rearrange("p a b -> p (a b)"), bk_[rs_, :], [bb_], [otB])
                pu = [pbt(), pbt()]
                for h in range(8):
                    bk_, bb_ = pu[h // 4]
                    sl = slice((h % 4) * 128, (h % 4) * 128 + 128)
                    mm(bk_[:, sl], KDEC[rs_, h, :], U[rs_, h, :], True, True, [kdB, uB], [bb_])
                for h in range(8):
                    bk_, bb_ = pu[h // 4]
                    sl = slice((h % 4) * 128, (h % 4) * 128 + 128)
                    stt("dve", S32[:, h, :], S32[:, h, :], EGL[:, h, ck:ck + 1], bk_[:, sl], ALU.mult, ALU.add, [s32B[h], eglB, bb_], [s32B[h]])
                cp("act", SBF[:, :, :], S32[:, :, :], s32B, [sbfB])
                yield
            tt("dve", scr[:, :, :], Otm[:, :, :], Otm[:, :, :], ALU.mult, [otB], [scrB])
            S.op("dve", lambda e: e.tensor_reduce(out=SS[:, :], in_=scr[:, :, :], axis=AX.X, op=ALU.add), reads=[scrB], writes=[ssB])
            ts("dve", SS[:, :], SS[:, :], 1.0 / 128, EPS, ALU.mult, ALU.add, [ssB], [ssB])
            act(SS[:, :], SS[:, :], AF.Sqrt, [ssB], [ssB])
            S.op("dve", lambda e: e.reciprocal(out=SS[:, :], in_=SS[:, :]), reads=[ssB], writes=[ssB])
            tt("dve", ON[:, :, :], Otm[:, :, :], SS[:, :].unsqueeze(2).to_broadcast([128, 8, 128]), ALU.mult, [otB, ssB], [onB])
            for hh in range(2):
                bk, bb = pbt()
                for j in range(4):
                    h = hh * 4 + j
                    mm(bk[:, j * 128:(j + 1) * 128], ON[:, h, :], idb[:, :], True, True, [onB, idbB], [bb])
                for j in range(4):
                    h = hh * 4 + j
                    stt("dve", ZSt[:, h, :], bk[:, j * 128:(j + 1) * 128], par[:, P_ONW:P_ONW + 1], ZSt[:, h, :],
                        ALU.mult, ALU.mult, [bb, parB, zB[p][h]], [zB[p][h]])
            yield

        def outproj_gen(bi):
            p = bi % 2
            c0 = bi * DB
            for dc in range(8):
                k = woc[0] % 2
                woc[0] += 1
                dma("pool", wo[k][:, :, :].rearrange("p a b -> p (a b)"), wout[dc], ("wo", k), wr=[woB[k]])
                bk, bb = pbt()
                for hc in range(8):
                    mm(bk[:, 0:DB], wo[k][:, hc, :], OG[p][:, hc, 0:DB], hc == 0, hc == 7, [woB[k], ogB[p][hc]], [bb], sig=(hc == 7))
                tt("dve", xT[:, dc, c0:c0 + DB], bk[:, 0:DB], xT[:, dc, c0:c0 + DB], ALU.add, [bb, xD[dc][bi]], [xD[dc][bi]])
                if dc % 2 == 1:
                    yield

        def block_gen(bi):
            yield from tile_gen(bi, 0)
            yield from tile_gen(bi, 1)
            yield from outproj_gen(bi)

        def drive(gens):
            gens = list(gens)
            while gens:
                for g in list(gens):
                    try:
                        next(g)
                    except StopIteration:
                        gens.remove(g)

        drive([proj_gen(0)])
        for bi in range(NB):
            gs = [block_gen(bi)]
            if bi + 1 < NB:
                gs.append(proj_gen(bi + 1))
            drive(gs)

        S.barrier()
        S.emit(lat=0.3, cp=True)
        sa.close()
        p = (NB - 1) % 2
        QT = QK[p][:, 0:8, :]
        KT = QK[p][:, 8:16, :]
        pb = pbs
        dma("sp", ssm_p.rearrange("h d v -> d h v"), S32[:, :, :], "st_ssmp", rd=s32B)
        dma("sp", dnc_new.rearrange("(c p) k -> p c k", p=128), newq[:, :, :], "st_dnc", rd=nqB)
        with contextlib.ExitStack() as sb_:
            K32 = sbt(sb_, "K32", [128, 8, 16], F32); Q32 = sbt(sb_, "Q32", [128, 8, 16], F32); V32 = sbt(sb_, "V32", [128, 8, 16], F32)
            k32B = Buf("k32")
            KM = sbt(sb_, "KM", [128, 8, 16, 16], BF16); QM = sbt(sb_, "QM", [128, 8, 16, 16], BF16)
            kmB, qmB = Buf("km"), Buf("qm")
            EGs = sbt(sb_, "EGs", [8, 16], F32); egsB = Buf("egs")
            SCS = sbt(sb_, "SCS", [16, 16], F32); scsB = Buf("scs")
            VTM = sbt(sb_, "VTM", [16, 8, 128], F32); KTM = sbt(sb_, "KTM", [16, 8, 128], BF16)
            vtmB, ktmB = Buf("vtm"), Buf("ktm")
            EGD = sbt(sb_, "EGD", [16, 16, 8], F32); egdB = Buf("egd")
            EGS = sbt(sb_, "EGS", [128, 16, 8], F32); egSB = Buf("egS")
            Us = sbt(sb_, "Us", [16, 8, 128], F32); usB2 = Buf("us")
            UM = sbt(sb_, "UM", [16, 8, 128], BF16); umB = Buf("um")
            SbH = [sbt(sb_, f"SbH{i}", [128, 8, 128], BF16) for i in range(2)]
            sbhB = [Buf("sbh0"), Buf("sbh1")]
            SnH = [sbt(sb_, f"SnH{i}", [128, 8, 128], BF16) for i in range(2)]
            snhB = [Buf("snh0"), Buf("snh1")]
            NSB = 3
            Sb = [sbt(sb_, f"Sb{i}", [128, 8, 128], F32) for i in range(NSB)]
            sbB = [Buf(f"sb{i}") for i in range(NSB)]
            Sn = [sbt(sb_, f"Sn{i}", [128, 8, 128], F32) for i in range(3)]
            snB = [Buf("sn0"), Buf("sn1"), Buf("sn2")]
            scr2 = sbt(sb_, "scr2", [16, 8, 128], F32); scr2B = Buf("scr2")
            ZL = sbt(sb_, "ZL", [128, 16], F32); zlB = Buf("zl")
            ON = sbt(sb_, "ONs", [16, 8, 128], BF16); onB = Buf("ons")
            memset("dve", ZL[:, :], 0.0, [zlB])
            cp("dve", K32[:, :, :], KT[:, :, DB:DB + 16], qkB[p], [k32B])
            cp("dve", Q32[:, :, :], QT[:, :, DB:DB + 16], qkB[p], [k32B])
            cp("dve", V32[:, :, :], VT[p][:, :, DB:DB + 16], vB[p], [k32B])
            i16 = con[:, C_I16:C_I16 + 256].rearrange("p (a b) -> p a b", a=16)
            for h in range(8):
                tt("dve", KM[:, h, :, :], K32[:, h, :].unsqueeze(2).to_broadcast([128, 16, 16]), i16, ALU.mult, [k32B, conB], [kmB])
                tt("dve", QM[:, h, :, :], Q32[:, h, :].unsqueeze(2).to_broadcast([128, 16, 16]), i16, ALU.mult, [k32B, conB], [qmB])
            act(EGs[:, :], GA[p][:, DB:DB + 16], AF.Exp, [gaB[p]], [egsB])
            bk, bb = pb()
            mm(bk[0:16, 0:8], EGs[:, :], con[0:8, C_ID:C_ID + 8], True, True, [egsB, conB], [bb])
            mm(bk[0:16, 8:16], GB[p][:, DB:DB + 16], con[0:8, C_ID:C_ID + 8], True, True, [gbB[p], conB], [bb])
            cp("dve", SCS[:, :], bk[0:16, 0:16], [bb], [scsB])
            for (src, dstt, dB_) in ((V32, VTM, vtmB), (K32, KTM, ktmB)):
                for hh in range(2):
                    bk, bb = pb()
                    for j in range(4):
                        mm(bk[0:16, j * 128:(j + 1) * 128], src[:, hh * 4 + j, :], idf, True, True, [k32B, conB], [bb])
                    cp("dve", dstt[:, hh * 4:hh * 4 + 4, :].rearrange("p a b -> p (a b)"), bk[0:16, :], [bb], [dB_])
            tt("dve", EGD[:, :, :], SCS[:, 0:8].unsqueeze(1).to_broadcast([16, 16, 8]),
               con[0:16, C_ID:C_ID + 16].unsqueeze(2).to_broadcast([16, 16, 8]), ALU.mult, [scsB, conB], [egdB])
            bk, bb = pb()
            mm(bk[:, 0:128], con[0:16, C_ONE:C_ONE + 128], EGD[:, :, :].rearrange("p a b -> p (a b)"), True, True, [egdB, conB], [bb])
            cp("dve", EGS[:, :, :].rearrange("p a b -> p (a b)"), bk[:, 0:128], [bb], [egSB])
            KSA, ksaB = scr2, scr2B
            OSA, osaB = Otm[0:16, :, :], otB
            memset("dve", KSA[:, :, :], 0.0, [ksaB])
            memset("dve", OSA[:, :, :], 0.0, [osaB])
            for b in range(16):
                dma("sp", Sb[b % NSB][:, :, :].rearrange("p a b -> p (a b)"), sstate[b], ("ld_S", b % NSB), wr=[sbB[b % NSB]])
                cp("act", SbH[b % 2][:, :, :], Sb[b % NSB][:, :, :], [sbB[b % NSB]], [sbhB[b % 2]])
                pk = [pb(), pb()]
                for h in range(8):
                    bk_, bb_ = pk[h // 4]
                    mm(bk_[0:16, (h % 4) * 128:(h % 4) * 128 + 128], KM[:, h, b, :], SbH[b % 2][:, h, :], True, True, [kmB, sbhB[b % 2]], [bb_])
                for q2 in range(2):
                    bk_, bb_ = pk[q2]
                    tt("dve", KSA[:, q2 * 4:q2 * 4 + 4, :], KSA[:, q2 * 4:q2 * 4 + 4, :], bk_[0:16, :].rearrange("p (a b) -> p a b", a=4),
                       ALU.add, [ksaB, bb_], [ksaB])
            tt("dve", scr2[:, :, :], KSA[:, :, :], SCS[:, 0:8].unsqueeze(2).to_broadcast([16, 8, 128]), ALU.mult, [ksaB, scsB], [scr2B])
            tt("dve", scr2[:, :, :], VTM[:, :, :], scr2[:, :, :], ALU.subtract, [vtmB, scr2B], [scr2B])
            tt("dve", Us[:, :, :], scr2[:, :, :], SCS[:, 8:16].unsqueeze(2).to_broadcast([16, 8, 128]), ALU.mult, [scr2B, scsB], [usB2])
            for b in range(16):
                sl_ = b % 2
                sl3 = (16 + b) % NSB
                sn3 = b % 3
                dma("sp", Sb[sl3][:, :, :].rearrange("p a b -> p (a b)"), sstate[b], ("ld_S", sl3), wr=[sbB[sl3]])
                ts("dve", UM[:, :, :], Us[:, :, :], con[0:16, C_ID + b:C_ID + b + 1], None, ALU.mult, None, [usB2, conB], [umB])
                pu = [pb(), pb()]
                for h in range(8):
                    bk_, bb_ = pu[h // 4]
                    mm(bk_[:, (h % 4) * 128:(h % 4) * 128 + 128], KTM[:, h, :], UM[:, h, :], True, True, [ktmB, umB], [bb_])
                for h in range(8):
                    bk_, bb_ = pu[h // 4]
                    stt("dve", Sn[sn3][:, h, :], Sb[sl3][:, h, :], EGS[:, b, h:h + 1], bk_[:, (h % 4) * 128:(h % 4) * 128 + 128],
                        ALU.mult, ALU.add, [sbB[sl3], egSB, bb_], [snB[sn3]])
                dma("sp", ssm_s[b].rearrange("h d v -> d h v"), Sn[sn3][:, :, :], ("st_S", sn3), rd=[snB[sn3]])
                cp("act", SnH[sl_][:, :, :], Sn[sn3][:, :, :], [snB[sn3]], [snhB[sl_]])
                po_ = [pb(), pb()]
                for h in range(8):
                    bk_, bb_ = po_[h // 4]
                    mm(bk_[0:16, (h % 4) * 128:(h % 4) * 128 + 128], QM[:, h, b, :], SnH[sl_][:, h, :], True, True, [qmB, snhB[sl_]], [bb_])
                for q2 in range(2):
                    bk_, bb_ = po_[q2]
                    tt("dve", OSA[:, q2 * 4:q2 * 4 + 4, :], OSA[:, q2 * 4:q2 * 4 + 4, :], bk_[0:16, :].rearrange("p (a b) -> p a b", a=4),
                       ALU.add, [osaB, bb_], [osaB])
            R_ = 16
            tt("dve", scr2[0:R_, :, :], Otm[0:R_, :, :], Otm[0:R_, :, :], ALU.mult, [otB], [scr2B])
            S.op("dve", lambda e: e.tensor_reduce(out=SS[0:16, :], in_=scr2[0:16, :, :], axis=AX.X, op=ALU.add), reads=[scr2B], writes=[ssB])
            ts("dve", SS[0:R_, :], SS[0:R_, :], 1.0 / 128, EPS, ALU.mult, ALU.add, [ssB], [ssB])
            act(SS[0:R_, :], SS[0:R_, :], AF.Sqrt, [ssB], [ssB])
            S.op("dve", lambda e: e.reciprocal(out=SS[0:16, :], in_=SS[0:16, :]), reads=[ssB], writes=[ssB])
            tt("dve", ON[0:R_, :, :], Otm[0:R_, :, :], SS[0:R_, :].unsqueeze(2).to_broadcast([R_, 8, 128]), ALU.mult, [otB, ssB], [onB])
            for hh in range(2):
                bk, bb = pb()
                for j in range(4):
                    h = hh * 4 + j
                    mm(bk[:, j * 128:j * 128 + R_], ON[0:R_, h, :], idb[0:R_, 0:R_], True, True, [onB, idbB], [bb])
                for j in range(4):
                    h = hh * 4 + j
                    stt("dve", OG[p][:, h, DB:DB + 16], bk[:, j * 128:j * 128 + R_], par[:, P_ONW:P_ONW + 1], ZS[p][:, h, DB:DB + 16],
                        ALU.mult, ALU.mult, [bb, parB, zB[p][h]], [ogB[p][h]])
            for dc in range(8):
                k = woc[0] % 2
                woc[0] += 1
                dma("pool", wo[k][:, :, :].rearrange("p a b -> p (a b)"), wout[dc], ("wo", k), wr=[woB[k]])
                bk, bb = pb()
                for hc in range(8):
                    mm(bk[:, 0:16], wo[k][:, hc, :], OG[p][:, hc, DB:DB + 16], hc == 0, hc == 7, [woB[k], ogB[p][hc]], [bb], sig=(hc == 7))
                tt("dve", xT[:, dc, T:NT], bk[:, 0:16], xT[:, dc, T:NT], ALU.add, [bb, xD[dc][NB]], [xD[dc][NB]])
            S.barrier()
            S.emit(lat=1.0)
        S.barrier()
        S.emit()

import numpy as np


def make_consts():
    con = np.zeros((128, NCON), np.float32)
    con[:, C_ID:C_ID + 128] = np.eye(128, dtype=np.float32)
    i = np.arange(128)[:, None]
    j = np.arange(128)[None, :]
    same = (i // 64) == (j // 64)
    con[:, C_ML:C_ML + 128] = np.where(same & (i > j), 0.0, 30000.0).astype(np.float32)
    con[:, C_MU:C_MU + 128] = np.where(same & (j >= i), 0.0, -30000.0).astype(np.float32)
    t = np.arange(128)
    con[:, C_CM:C_CM + 128] = (t % 64 != 0).astype(np.float32)[None, :]
    for g in range(4):
        w = 2 ** (g + 1)
        pos = np.arange(16)
        con[:, C_RT + g * 16:C_RT + g * 16 + 16] = (w / np.minimum(w, pos + 1)).astype(np.float32)[None, :]
    con[:, C_ONE:C_ONE + 128] = 1.0
    con[:, C_I16:C_I16 + 256] = np.eye(16, dtype=np.float32).reshape(1, 256)
    return con


def colvec(v, n):
    return np.ascontiguousarray(v.reshape(n, 128).T)


def make_shared(inp):
    f = lambda k: np.asarray(inp[k], np.float32)
    sh = {}
    wu = f("ffn_w_up").reshape(2, 8, 128, 2, NPAIR, 128)
    sh["wup"] = np.ascontiguousarray(wu.transpose(0, 4, 2, 1, 3, 5)).reshape(2, NPAIR, 128, 2048)
    wd = f("ffn_w_down").reshape(2, NPAIR, 128, 8, 128)
    sh["wdn"] = np.ascontiguousarray(wd.transpose(0, 3, 2, 1, 4)).reshape(2, 8, 128, NPAIR * 128)
    wi = np.zeros((1024, 33 * 128), np.float32)
    wi[:, :4112] = f("dn_w_in")[0]
    wi = wi.reshape(8, 128, 33, 128)
    sh["win"] = np.ascontiguousarray(wi.transpose(2, 1, 0, 3)).reshape(33, 128, 1024)
    wo = f("dn_w_out")[0].reshape(8, 128, 8, 128)
    sh["wout"] = np.ascontiguousarray(wo.transpose(2, 1, 0, 3)).reshape(8, 128, 1024)
    pw = f("pool_w")[0].reshape(4, 2, 128, 256)
    sh["poolw"] = np.ascontiguousarray(pw.transpose(2, 0, 1, 3)).reshape(128, 2048)
    par = np.zeros((128, NPAR), np.float32)
    for l in range(2):
        par[:, P_N1 + 8 * l:P_N1 + 8 * l + 8] = colvec(f("norm1_w")[l], 8)
        par[:, P_N2 + 8 * l:P_N2 + 8 * l + 8] = colvec(f("norm2_w")[l], 8)
        for jj in range(3):
            par[:, P_FCW + (l * 3 + jj) * 44:P_FCW + (l * 3 + jj) * 44 + 44] = colvec(f("ffn_conv_w")[l, jj], 44)
        par[:, P_FCB + l * 44:P_FCB + l * 44 + 44] = colvec(f("ffn_conv_b")[l], 44)
    par[:, P_FN:P_FN + 8] = colvec(f("final_norm_w"), 8)
    par[:, P_PS:P_PS + 8] = colvec(f("pool_scale")[0], 8)
    for jj in range(4):
        par[:, P_DCW + jj * 24:P_DCW + jj * 24 + 24] = colvec(f("dn_conv_w")[0, jj], 24)
    par[:, P_ONW] = f("dn_o_norm_w")[0]
    par[0:8, P_AL] = f("dn_a_log")[0]
    par[0:8, P_DT] = f("dn_dt_bias")[0]
    sh["params"] = par
    sh["consts"] = make_consts()
    return sh


def make_core(inp, c):
    f = lambda k: np.asarray(inp[k], np.float32)
    m = {}
    bs = slice(16 * c, 16 * c + 16)
    m["xin"] = np.ascontiguousarray(np.concatenate([f("x_prompt")[c].T, f("x_sample")[bs, 0, :].T], axis=1))
    ps = f("state_pool_buf")[0, bs].reshape(16, 15, 8, 128)
    m["pstate"] = np.ascontiguousarray(ps.transpose(3, 2, 0, 1)).reshape(128, 8 * 16 * 15)
    fs = f("state_ffn_conv")[:, bs].reshape(2, 16, 2, 44, 128)
    m["fstate"] = np.ascontiguousarray(fs.transpose(0, 4, 3, 1, 2)).reshape(2, 128, 44 * 32)
    ds = f("state_dn_conv")[0, bs].reshape(16, 3, 24, 128)
    m["dcstate"] = np.ascontiguousarray(ds.transpose(3, 2, 0, 1)).reshape(128, 24 * 48)
    ss = f("state_dn_ssm")[0, bs]
    m["sstate"] = np.ascontiguousarray(ss.transpose(0, 2, 1, 3)).reshape(16, 128, 1024)
    m["pool_raw"] = np.ascontiguousarray(f("state_pool_buf")[0, bs].reshape(16, 15 * 1024))
    m["f_raw"] = np.ascontiguousarray(f("state_ffn_conv")[:, bs].reshape(2, 16, 2 * 5632))
    m["dc_raw"] = np.ascontiguousarray(f("state_dn_conv")[0, bs].reshape(16, 3 * 3072))
    return m


def assemble(results, ncores=8):
    y_p = np.zeros((8, T, D), np.float32)
    y_s = np.zeros((128, 1, D), np.float32)
    pool_p = np.zeros((1, 8, 15, D), np.float32)
    pool_s = np.zeros((1, 128, 15, D), np.float32)
    dnc_p = np.zeros((1, 8, 3, 3072), np.float32)
    dnc_s = np.zeros((1, 128, 3, 3072), np.float32)
    dns_p = np.zeros((1, 8, 8, 128, 128), np.float32)
    dns_s = np.zeros((1, 128, 8, 128, 128), np.float32)
    ffn_p = np.zeros((2, 8, 2, 5632), np.float32)
    ffn_s = np.zeros((2, 128, 2, 5632), np.float32)
    for c in range(ncores):
        r = results[c]
        bs = slice(16 * c, 16 * c + 16)
        y_p[c] = r["yT"][:, :T].T
        y_s[bs, 0] = r["yT"][:, T:].T
        pool_p[0, c] = r["pool_new"][:, 0:15].T
        pool_s[0, bs, 0:14] = r["pool_old"].reshape(16, 14, D)
        pool_s[0, bs, 14] = r["pool_new"][:, 15:31].T
        dnc_p[0, c] = r["dnc_new"][:, 0:3].T
        dnc_s[0, bs, 0:2] = r["dnc_old"].reshape(16, 2, 3072)
        dnc_s[0, bs, 2] = r["dnc_new"][:, 3:19].T
        dns_p[0, c] = r["ssm_p"]
        dns_s[0, bs] = r["ssm_s"]
        for l in range(2):
            ffn_p[l, c] = r["ffn_new"][l][:, 0:2].T
            ffn_s[l, bs, 0] = r["ffn_old"][l]
            ffn_s[l, bs, 1] = r["ffn_new"][l][:, 2:18].T
    return (y_p, y_s, pool_p, pool_s, dnc_p, dnc_s, dns_p, dns_s, ffn_p, ffn_s)


_NC_CACHE = {}


def kernel(**inputs):
    inp = {k: np.asarray(v) for k, v in inputs.items()}
    if "nc" not in _NC_CACHE:
        _NC_CACHE["nc"] = build_program(True, dn_hook)
    nc = _NC_CACHE["nc"]
    sh = make_shared(inp)
    in_maps = []
    for c in range(8):
        m = dict(sh)
        m.update(make_core(inp, c))
        in_maps.append(m)
    res = run_bass_kernel_spmd(nc, in_maps, core_ids=list(range(8)))
    return assemble(res.results, ncores=8)
```

```python
import contextlib
import heapq
import numpy as np
import concourse.bass as bass
import concourse.mybir as mybir
from concourse.bass_utils import run_bass_kernel_spmd

F32 = mybir.dt.float32
BF16 = mybir.dt.bfloat16
ALU = mybir.AluOpType
AF = mybir.ActivationFunctionType
AX = mybir.AxisListType


class Buf:
    __slots__ = ("name", "w", "r")

    def __init__(self, name=""):
        self.name = name
        self.w = None
        self.r = []


class Node:
    __slots__ = ("eng", "fn", "sem", "inc", "deps", "dur", "lat", "idx", "tick", "users_x", "phase", "fin", "nd", "outs")

    def __init__(self, eng, fn, sem, inc, dur, lat, idx, phase):
        self.eng = eng
        self.fn = fn
        self.sem = sem
        self.inc = inc
        self.deps = []
        self.dur = dur
        self.lat = lat
        self.idx = idx
        self.tick = None
        self.users_x = False
        self.phase = phase
        self.fin = 0.0
        self.nd = 0
        self.outs = []


class Sched:
    ENG = ("pe", "act", "dve", "pool", "sp")
    SEM_LAT = 0.6
    REORDER = True
    CP = False

    def __init__(self, nc, es):
        self.nc = nc
        self.es = es
        self.nodes = []
        self.cnt = {e: 0 for e in self.ENG}
        self.seen = {e: {} for e in self.ENG}
        self.dma_cnt = {}
        self.last_dma = {}
        self.sem = {}
        self.phase = 0
        self.tail = {e: [] for e in self.ENG}
        self.nidx = 0
        for e in self.ENG:
            self._sem(e)

    def _sem(self, k):
        if k not in self.sem:
            nm = "s_" + "_".join(str(x) for x in (k if isinstance(k, tuple) else (k,)))
            self.sem[k] = self.es.enter_context(self.nc.semaphore(nm))
        return self.sem[k]

    def _add(self, node, reads, writes):
        ph = self.phase
        deps = node.deps
        for b in reads:
            w = b.w
            if w is not None and w.phase == ph:
                deps.append(w)
        for b in writes:
            w = b.w
            if w is not None and w.phase == ph:
                deps.append(w)
            for r in b.r:
                if r.phase == ph:
                    deps.append(r)
        for b in reads:
            b.r.append(node)
        for b in writes:
            b.w = node
            b.r = []
        self.nodes.append(node)

    def op(self, eng, fn, reads=(), writes=(), sig=True, dur=0.3):
        self.nidx += 1
        n = Node(eng, fn, eng, 1, dur, dur, self.nidx, self.phase)
        self._add(n, reads, writes)

    def dma(self, q, fn, semkey, reads=(), writes=(), nbytes=65536):
        self._sem(semkey)
        self.nidx += 1
        issue = 0.7 if q == "pool" else 0.15
        n = Node(q, fn, semkey, 16, issue, 2.2 + nbytes / 150e3, self.nidx, self.phase)
        prev = self.last_dma.get(semkey)
        if prev is not None and prev.phase == self.phase:
            n.deps.append(prev)
        self.last_dma[semkey] = n
        self._add(n, reads, writes)

    def _schedule(self, nodes):
        if not self.REORDER:
            order = {e: [] for e in self.ENG}
            for n in nodes:
                order[n.eng].append(n)
            return order
        for n in nodes:
            n.deps = list({id(d): d for d in n.deps}.values())
            n.nd = len(n.deps)
            n.outs = []
        for n in nodes:
            for d in n.deps:
                d.outs.append(n)
        lat0 = getattr(self, '_lat', self.SEM_LAT)
        cpl = {}
        for n in reversed(nodes):
            m = 0.0
            for o in n.outs:
                v = cpl[id(o)] + lat0
                if v > m:
                    m = v
            cpl[id(n)] = n.lat + m
        if getattr(self, '_cp', self.CP):
            for n in nodes:
                n.idx = (-cpl[id(n)], n.idx)
        free = {e: 0.0 for e in self.ENG}
        fut = {e: [] for e in self.ENG}
        now = {e: [] for e in self.ENG}
        rt = {}
        for n in nodes:
            if n.nd == 0:
                heapq.heappush(fut[n.eng], (0.0, n.idx, n))
        order = {e: [] for e in self.ENG}
        left = len(nodes)
        lat = getattr(self, '_lat', self.SEM_LAT)
        while left:
            best = None
            for e in self.ENG:
                f, fu, nw = free[e], fut[e], now[e]
                while fu and fu[0][0] <= f:
                    _, i, n = heapq.heappop(fu)
                    heapq.heappush(nw, (i, n))
                if nw:
                    cand = (f, nw[0][0], e, 0)
                elif fu:
                    cand = (fu[0][0], fu[0][1], e, 1)
                else:
                    continue
                if best is None or cand < best:
                    best = cand
            start, _, e, src = best
            if src == 0:
                _, n = heapq.heappop(now[e])
            else:
                _, _, n = heapq.heappop(fut[e])
            order[e].append(n)
            free[e] = start + n.dur
            n.fin = start + n.lat
            left -= 1
            for o in n.outs:
                o.nd -= 1
                r = rt.get(id(o), 0.0)
                t = n.fin + (0.0 if (n.eng == "pe" and o.eng == "pe") else lat)
                if t > r:
                    r = t
                rt[id(o)] = r
                if o.nd == 0:
                    heapq.heappush(fut[o.eng], (r, o.idx, o))
        return order

    def barrier(self):
        pass

    def finish(self, eng="sp"):
        pass

    def emit(self, final=False, lat=None, cp=None):
        self._cp = self.CP if cp is None else cp
        if lat is not None:
            self._lat = lat
        else:
            self._lat = self.SEM_LAT
        nodes = self.nodes
        self.nodes = []
        order = self._schedule(nodes)
        for n in nodes:
            for d in n.deps:
                if not (d.eng == "pe" and n.eng == "pe"):
                    d.users_x = True
        prog = {e: [] for e in self.ENG}
        for e in self.ENG:
            lst = order[e]
            last_sig = None
            for n in lst:
                if n.sem == e:
                    if n.eng != "pe" or n.users_x:
                        self.cnt[e] += 1
                        n.tick = self.cnt[e]
                        last_sig = n
                    else:
                        n.tick = None
                else:
                    self.dma_cnt[n.sem] = self.dma_cnt.get(n.sem, 0) + 16
                    n.tick = self.dma_cnt[n.sem]
            if e == "pe":
                for n in reversed(lst):
                    if n.sem == e:
                        if n.tick is None:
                            self.cnt[e] += 1
                            n.tick = self.cnt[e]
                        break
        nxt = None
        for n in reversed(order["pe"]):
            if n.tick is not None:
                nxt = n.tick
            else:
                n.tick = -(nxt if nxt is not None else 0)
        for e in self.ENG:
            seen = self.seen[e]
            for n in order[e]:
                waits = []
                for d in n.deps:
                    if d.eng == "pe" and e == "pe":
                        continue
                    k, v = d.sem, abs(d.tick)
                    if seen.get(k, 0) >= v:
                        continue
                    seen[k] = v
                    waits.append((k, v))
                inc = n.sem if (n.tick is not None and n.tick > 0) else None
                prog[e].append((waits, n.fn, inc, n.inc))
        for e in self.ENG:
            waits = []
            for k in self.ENG:
                if k != e and self.cnt[k] > self.seen[e].get(k, 0):
                    waits.append((k, self.cnt[k]))
                    self.seen[e][k] = self.cnt[k]
            for k, v in self.dma_cnt.items():
                if self.seen[e].get(k, 0) < v:
                    waits.append((k, v))
                    self.seen[e][k] = v
            if waits:
                prog[e].append((waits, None, None, 0))
        self.phase += 1
        nc = self.nc
        sem = self.sem
        with nc.Block() as block:
            def run(name):
                pr = prog[name]

                def f(eng):
                    for waits, fn, inc, n in pr:
                        for k, v in waits:
                            eng.wait_ge(sem[k], v)
                        if fn is None:
                            continue
                        ins = fn(eng)
                        if inc is not None:
                            ins.then_inc(sem[inc], n)
                return f

            block.tensor(run("pe"))
            block.scalar(run("act"))
            block.vector(run("dve"))
            block.gpsimd(run("pool"))
            block.sync(run("sp"))

import contextlib
import numpy as np

D = 1024
T = 2048
NS = 16
NT = T + NS
DFF = 2816
NPAIR = 22
EPS = 1e-6
SBS = [(0, 512), (512, 512), (1024, 512), (1536, 512), (2048, 16)]
BLOCKS = [[0], [1], [2], [3, 4]]

P_N1, P_N2, P_FN, P_PS, P_FCW, P_FCB, P_DCW, P_ONW, P_AL, P_DT, NPAR = 0, 16, 32, 40, 48, 312, 400, 496, 497, 498, 500
C_ID, C_ML, C_MU, C_CM, C_RT, C_ONE, C_I16, NCON = 0, 128, 256, 384, 512, 576, 704, 960


def build_program(with_dn=True, dn_hook=None):
    nc = bass.Bass("TRN2", target_bir_lowering=False)

    def din(name, shape):
        return nc.dram_tensor(name, list(shape), F32, kind="ExternalInput").ap()

    def dout(name, shape):
        return nc.dram_tensor(name, list(shape), F32, kind="ExternalOutput").ap()

    xin = din("xin", [D, NT])
    wup = din("wup", [2, NPAIR, 128, 2048])
    wdn = din("wdn", [2, 8, 128, NPAIR * 128])
    win = din("win", [33, 128, 1024])
    wout = din("wout", [8, 128, 1024])
    poolw = din("poolw", [128, 2048])
    params = din("params", [128, NPAR])
    consts = din("consts", [128, NCON])
    pstate = din("pstate", [128, 8 * 16 * 15])
    fstate = din("fstate", [2, 128, 44 * 32])
    dcstate = din("dcstate", [128, 24 * 48])
    sstate = din("sstate", [16, 128, 1024])
    pool_raw = din("pool_raw", [16, 15 * 1024])
    f_raw = din("f_raw", [2, 16, 2 * 5632])
    dc_raw = din("dc_raw", [16, 3 * 3072])

    yT = dout("yT", [D, NT])
    pool_new = dout("pool_new", [D, 31])
    pool_old = dout("pool_old", [16, 14 * 1024])
    dnc_new = dout("dnc_new", [3072, 19])
    dnc_old = dout("dnc_old", [16, 2 * 3072])
    ffn_new = dout("ffn_new", [2, 5632, 18])
    ffn_old = dout("ffn_old", [2, 16, 5632])
    ssm_p = dout("ssm_p", [8, 128, 128])
    ssm_s = dout("ssm_s", [16, 8, 128, 128])

    with contextlib.ExitStack() as es:
        S = Sched(nc, es)

        uniq = [0]

        def sbt(stack, name, shape, dt):
            uniq[0] += 1
            return stack.enter_context(nc.sbuf_tensor(f"{name}_{uniq[0]}", list(shape), dt))

        xT = sbt(es, "xT", [128, 8, NT], F32)
        par = sbt(es, "par", [128, NPAR], F32)
        con = sbt(es, "con", [128, NCON], F32)
        idb = sbt(es, "idb", [128, 128], BF16)
        RSP = sbt(es, "RSP", [128, NT], F32)
        SQP = sbt(es, "SQP", [128, 4, 512], BF16)
        rsPB = [Buf(f"rsp{s}") for s in range(5)]
        sqPB = [Buf(f"sqp{i}") for i in range(4)]
        sqc = [0]
        oneb = sbt(es, "oneb", [128, 128], BF16)
        xB = [[Buf(f"x{c}_{s}") for s in range(5)] for c in range(8)]
        nB = [[Buf(f"n{c}_{s}") for s in range(5)] for c in range(8)]
        parB, conB, idbB = Buf("par"), Buf("con"), Buf("idb")
        banks = [es.enter_context(nc.psum_tensor(f"pb{i}", [128, 512], F32)) for i in range(8)]
        bankB = [Buf(f"pb{i}") for i in range(8)]
        bctr = [0]

        def pb():
            i = bctr[0] % 8
            bctr[0] += 1
            return banks[i], bankB[i]

        def _dv(ap):
            return ap.free_size() / 960.0 + 0.15

        def _da(ap):
            return ap.free_size() / 1200.0 + 0.22

        def _dp(ap):
            return ap.free_size() * 2.2 / 1200.0 + 0.3

        def _de(eng, ap):
            return _dv(ap) if eng == "dve" else (_da(ap) if eng == "act" else _dp(ap))

        def mm(out, lhsT, rhs, start, stop, rd, wr, sig=True):
            passes = 4 if lhsT.dtype == F32 else 1
            S.op("pe", lambda e: e.matmul(out, lhsT=lhsT, rhs=rhs, start=start, stop=stop), reads=rd, writes=wr,
                 dur=max(0.11, out.free_size() * passes / 1900.0))

        def tt(eng, out, in0, in1, op, rd, wr):
            S.op(eng, lambda e: e.tensor_tensor(out=out, in0=in0, in1=in1, op=op), reads=rd, writes=wr, dur=_de(eng, out))

        def stt(eng, out, in0, scalar, in1, op0, op1, rd, wr):
            S.op(eng, lambda e: e.scalar_tensor_tensor(out=out, in0=in0, scalar=scalar, in1=in1, op0=op0, op1=op1), reads=rd, writes=wr,
                 dur=_de(eng, out))

        def ts(eng, out, in0, s1, s2, op0, op1, rd, wr):
            if s2 is None:
                S.op(eng, lambda e: e.tensor_scalar(out=out, in0=in0, scalar1=s1, scalar2=None, op0=op0), reads=rd, writes=wr, dur=_de(eng, out))
            else:
                S.op(eng, lambda e: e.tensor_scalar(out=out, in0=in0, scalar1=s1, scalar2=s2, op0=op0, op1=op1), reads=rd, writes=wr,
                     dur=_de(eng, out))

        def act(out, in_, func, rd, wr, **kw):
            S.op("act", lambda e: e.activation(out=out, in_=in_, func=func, **kw), reads=rd, writes=wr, dur=_da(out))

        def cp(eng, out, in_, rd, wr):
            if eng == "act":
                S.op("act", lambda e: e.activation(out=out, in_=in_, func=AF.Copy), reads=rd, writes=wr, dur=_da(out))
            else:
                S.op(eng, lambda e: e.tensor_copy(out=out, in_=in_), reads=rd, writes=wr, dur=_de(eng, out))

        def memset(eng, ap, val, wr):
            S.op(eng, lambda e: e.memset(ap, val), writes=wr, dur=ap.free_size() / 1900.0 + 0.1)

        def dma(q, out, in_, key, rd=(), wr=()):
            S.dma(q, lambda e: e.dma_start(out=out, in_=in_), key, reads=rd, writes=wr, nbytes=max(out.nbytes(), in_.nbytes()))

        def allx(c):
            return xB[c]

        H = dict(nc=nc, S=S, sbt=sbt, xT=xT, par=par, con=con, idb=idb, oneb=oneb, xB=xB, nB=nB,
                 parB=parB, conB=conB, idbB=idbB, pb=pb, mm=mm, tt=tt, stt=stt, ts=ts, act=act, cp=cp,
                 memset=memset, dma=dma, win=win, wout=wout, dcstate=dcstate, sstate=sstate, dc_raw=dc_raw,
                 dnc_new=dnc_new, dnc_old=dnc_old, ssm_p=ssm_p, ssm_s=ssm_s, banks=banks, bankB=bankB)

        dma("sp", par[:, :], params, "ld_par", wr=[parB])
        dma("sp", con[:, :], consts, "ld_con", wr=[conB])
        xin3 = xin.rearrange("(c p) t -> p c t", p=128)
        for s, (o, n) in enumerate(SBS):
            dma("sp", xT[:, :, o:o + n], xin3[:, :, o:o + n], ("ldx", s), wr=[xB[c][s] for c in range(8)])
        cp("dve", idb[:, :], con[:, C_ID:C_ID + 128], [conB], [idbB])
        cp("dve", oneb[:, :], con[:, C_ONE:C_ONE + 128], [conB], [idbB])
        dma("sp", pool_old, pool_raw[:, 1024:15 * 1024], "d2d")
        dma("sp", dnc_old, dc_raw[:, 3072:3 * 3072], "d2d")
        for l in range(2):
            dma("sp", ffn_old[l], f_raw[l][:, 5632:2 * 5632], "d2d")

        def emit_stats(bank_fn, all_act=False, subs=(0, 1, 2, 3, 4)):
            for s in subs:
                o, n = SBS[s]
                bk, bb = bank_fn()
                for c in range(8):
                    slot = sqc[0] % 4
                    sqc[0] += 1
                    sq = SQP[:, slot, 0:n]
                    if c % 2 == 0 or all_act:
                        act(sq, xT[:, c, o:o + n], AF.Square, [xB[c][s]], [sqPB[slot]])
                    else:
                        tt("dve", sq, xT[:, c, o:o + n], xT[:, c, o:o + n], ALU.mult, [xB[c][s]], [sqPB[slot]])
                    mm(bk[:, 0:n], oneb[:, :], sq, c == 0, c == 7, [sqPB[slot], idbB], [bb])
                act(RSP[:, o:o + n], bk[:, 0:n], AF.Ln, [bb], [rsPB[s]], scale=1.0 / D, bias=EPS)
                act(RSP[:, o:o + n], RSP[:, o:o + n], AF.Exp, [rsPB[s]], [rsPB[s]], scale=-0.5)

        def apply_norm(wcol0, XN):
            for s, (o, n) in enumerate(SBS):
                for c in range(8):
                    stt("dve", XN[:, c, o:o + n], xT[:, c, o:o + n], par[:, wcol0 + c:wcol0 + c + 1], RSP[:, o:o + n], ALU.mult, ALU.mult,
                        [xB[c][s], rsPB[s], parB], [nB[c][s]])

        with contextlib.ExitStack() as st:
            XN = sbt(st, "XN", [128, 8, NT], BF16)
            emit_stats(pb, all_act=True)
            RS, rsB = RSP, None
            xnf = [sbt(st, f"xnf{i}", [128, 16 + NT], F32) for i in range(2)]
            xfB = [Buf("xf0"), Buf("xf1")]
            PB = sbt(st, "PB", [128, 16 + T], F32)
            PC = sbt(st, "PC", [128, 16 + T], F32)
            pbB, pcB = Buf("PB"), Buf("PC")
            pst = sbt(st, "pst", [128, 8, 16, 15], F32)
            pstB = Buf("pst")
            pw = sbt(st, "pw", [128, 4, 2, 256], BF16)
            pwB = Buf("pw")
            red = sbt(st, "red", [128, 16], F32)
            redB = Buf("red")
            dma("sp", pst[:, :, :, :].rearrange("p a b c -> p (a b c)"), pstate, "ld_pst", wr=[pstB])
            dma("pool", pw[:, :, :, :].rearrange("p a b c -> p (a b c)"), poolw, "ld_pw", wr=[pwB])
            for i in range(2):
                memset("dve", xnf[i][:, 0:16], 0.0, [xfB[i]])
            for c in range(8):
                g = c // 2
                w = 2 ** (g + 1)
                xf, xb_ = xnf[c % 2], xfB[c % 2]
                stt("dve", xf[:, 16:16 + NT], xT[:, c, :], par[:, P_N1 + c:P_N1 + c + 1], RSP[:, :], ALU.mult, ALU.mult,
                    xB[c] + rsPB + [parB], [xb_])
                dma("sp", pool_new[c * 128:(c + 1) * 128, :], xf[:, 16 + 2033:16 + NT], ("st_pool", c % 2), rd=[xb_])
                E_ = 16 + T
                weng = "dve"
                tt(weng, PB[:, 1:E_], xf[:, 1:E_], xf[:, 0:E_ - 1], ALU.add, [xb_], [pbB])
                s_ap, s_b = PB, pbB
                if g >= 1:
                    tt(weng, PC[:, 3:E_], PB[:, 3:E_], PB[:, 1:E_ - 2], ALU.add, [pbB], [pcB])
                    s_ap, s_b = PC, pcB
                if g >= 2:
                    tt(weng, PB[:, 7:E_], PC[:, 7:E_], PC[:, 3:E_ - 4], ALU.add, [pcB], [pbB])
                    s_ap, s_b = PB, pbB
                if g >= 3:
                    tt(weng, PC[:, 15:E_], PB[:, 15:E_], PB[:, 7:E_ - 8], ALU.add, [pbB], [pcB])
                    s_ap, s_b = PC, pcB
                tt("dve", s_ap[:, 16:32], s_ap[:, 16:32], con[:, C_RT + g * 16:C_RT + g * 16 + 16], ALU.mult, [s_b, conB], [s_b])
                stt("dve", XN[:, c, 0:T], s_ap[:, 16:E_], 1.0 / w, xf[:, 16:E_], ALU.mult, ALU.subtract,
                    [s_b, xb_], nB[c][0:4])
                S.op("dve", lambda e, c=c, w=w: e.tensor_reduce(out=red[:, :], in_=pst[:, c, :, 16 - w:15], axis=AX.X, op=ALU.add),
                     reads=[pstB], writes=[redB])
                tt("dve", red[:, :], red[:, :], xf[:, E_:16 + NT], ALU.add, [redB, xb_], [redB])
                stt("dve", XN[:, c, T:NT], red[:, :], 1.0 / w, xf[:, E_:16 + NT], ALU.mult, ALU.subtract,
                    [redB, xb_], [nB[c][4]])
            for g in range(4):
                for dc in range(2):
                    co = 2 * g + dc
                    for s, (o, n) in enumerate(SBS):
                        bk, bb = pb()
                        for cc in range(2):
                            mm(bk[:, 0:n], pw[:, g, cc, dc * 128:(dc + 1) * 128], XN[:, 2 * g + cc, o:o + n], cc == 0, cc == 1,
                               [pwB, nB[2 * g + cc][s]], [bb])
                        stt("dve", xT[:, co, o:o + n], bk[:, 0:n], par[:, P_PS + co:P_PS + co + 1], xT[:, co, o:o + n],
                            ALU.mult, ALU.add, [bb, parB, xB[co][s]], [xB[co][s]])
            emit_stats(pb, all_act=True)
            S.barrier()
            S.emit(lat=1.0)

        FBLK = [[0, 1], [2, 3, 4]]

        def ffn(l):
            with contextlib.ExitStack() as st:
                W = 1024
                XN = sbt(st, "XNf", [128, 8, W + 16], BF16)
                xnl = [[Buf(f"xnl{c}_{j}") for j in range(3)] for c in range(8)]
                if l == 1 and with_dn:
                    emit_stats(pb)
                ACTB = sbt(st, "ACTB", [128, NPAIR, W + 16], BF16)
                aB = [Buf(f"a{i}") for i in range(NPAIR)]
                NH = 4
                Hb = [sbt(st, f"H{i}", [128, 2 + W], F32) for i in range(NH)]
                hB = [Buf(f"h{i}") for i in range(NH)]
                TT = [sbt(st, f"TT{i}", [128, W], F32) for i in range(2)]
                ttB = [Buf("tt0"), Buf("tt1")]
                SG = sbt(st, "SG", [128, W + 16], BF16)
                sgB = Buf("sg")
                sm1 = [sbt(st, f"sm1_{i}", [128, 16], F32) for i in range(2)]
                s1B = [Buf("s10"), Buf("s11")]
                newh = sbt(st, "newh", [128, 44, 18], F32)
                nhB = [Buf(f"nh{i}") for i in range(44)]
                fst = sbt(st, "fst", [128, 44, 16, 2], F32)
                fstB = Buf("fst")
                NWU = 3
                wu = [sbt(st, f"wu{i}", [128, 8, 2, 128], BF16) for i in range(NWU)]
                wuB = [Buf(f"wu{i}") for i in range(NWU)]
                wd = [sbt(st, f"wd{i}", [128, NPAIR, 128], BF16) for i in range(2)]
                wdB = [Buf("wd0"), Buf("wd1")]
                dma("sp", fst[:, :, :, :].rearrange("p a b c -> p (a b c)"), fstate[l], "ld_fst", wr=[fstB])
                memset("dve", newh[:, :, 0:2], 0.0, nhB)
                cnt = {"wu": 0, "wd": 0, "u": 0}

                def pbu():
                    i = cnt["u"] % 8
                    cnt["u"] += 1
                    return banks[i], bankB[i]

                def load_wu(i):
                    k = cnt["wu"] % NWU
                    cnt["wu"] += 1
                    dma("pool", wu[k][:, :, :, :].rearrange("p a b c -> p (a b c)"), wup[l, i], ("wu", k), wr=[wuB[k]])
                    return k

                for bi, blk in enumerate(FBLK):
                    b0 = SBS[blk[0]][0]
                    has_s = 4 in blk
                    loc = {s: (SBS[s][0] - b0 if s < 4 else W) for s in blk}
                    jdx = {s: j for j, s in enumerate(blk)}
                    for s in blk:
                        o, n = SBS[s]
                        for c in range(8):
                            stt("dve", XN[:, c, loc[s]:loc[s] + n], xT[:, c, o:o + n], par[:, P_N2 + 8 * l + c:P_N2 + 8 * l + c + 1],
                                RSP[:, o:o + n], ALU.mult, ALU.mult, [xB[c][s], rsPB[s], parB], [xnl[c][jdx[s]]])
                    slots = {}
                    PRE = 2
                    for i in range(min(PRE, NPAIR)):
                        slots[i] = load_wu(i)

                    def stage_a(i):
                        if i + PRE < NPAIR:
                            slots[i + PRE] = load_wu(i + PRE)
                        k = slots[i]
                        for gv in range(2):
                            ch = gv * NPAIR + i
                            hi_ = (i % 2) * 2 + gv
                            Ht, hb_ = Hb[hi_], hB[hi_]
                            cp("act", Ht[:, 0:2], newh[:, ch, 0:2], [nhB[ch]], [hb_])
                            for s in blk:
                                o, n = SBS[s]
                                bk, bb = pbu()
                                for kc in range(8):
                                    mm(bk[:, 0:n], wu[k][:, kc, gv, :], XN[:, kc, loc[s]:loc[s] + n], kc == 0, kc == 7,
                                       [wuB[k], xnl[kc][jdx[s]]], [bb], sig=(kc == 7))
                                if s < 4:
                                    cp("act", Ht[:, 2 + o - b0:2 + o - b0 + n], bk[:, 0:n], [bb], [hb_])
                                else:
                                    cp("act", newh[:, ch, 2:18], bk[:, 0:16], [bb], [nhB[ch]])
                            cp("act", newh[:, ch, 0:2], Ht[:, W:W + 2], [hb_], [nhB[ch]])

                    def stage_b(i):
                        for gv in range(2):
                            ch = gv * NPAIR + i
                            hi_ = (i % 2) * 2 + gv
                            Ht, hb_ = Hb[hi_], hB[hi_]
                            w0 = par[:, P_FCW + (l * 3 + 0) * 44 + ch:P_FCW + (l * 3 + 0) * 44 + ch + 1]
                            w1 = par[:, P_FCW + (l * 3 + 1) * 44 + ch:P_FCW + (l * 3 + 1) * 44 + ch + 1]
                            w2 = par[:, P_FCW + (l * 3 + 2) * 44 + ch:P_FCW + (l * 3 + 2) * 44 + ch + 1]
                            bb_ = par[:, P_FCB + l * 44 + ch:P_FCB + l * 44 + ch + 1]
                            t_, tb_ = TT[gv], ttB[gv]
                            ts("pool", t_[:, 0:W], Ht[:, 2:2 + W], w2, bb_, ALU.mult, ALU.add, [hb_, parB], [tb_])
                            stt("dve", t_[:, 0:W], Ht[:, 1:1 + W], w1, t_[:, 0:W], ALU.mult, ALU.add, [hb_, parB, tb_], [tb_])
                            stt("dve", t_[:, 0:W], Ht[:, 0:W], w0, t_[:, 0:W], ALU.mult, ALU.add, [hb_, parB, tb_], [tb_])
                            if has_s:
                                a1, a1b = sm1[gv], s1B[gv]
                                ts("dve", a1[:, :], newh[:, ch, 2:18], w2, bb_, ALU.mult, ALU.add, [nhB[ch], parB], [a1b])
                                stt("dve", a1[:, :], fst[:, ch, :, 1], w1, a1[:, :], ALU.mult, ALU.add, [fstB, parB, a1b], [a1b])
                                stt("dve", a1[:, :], fst[:, ch, :, 0], w0, a1[:, :], ALU.mult, ALU.add, [fstB, parB, a1b], [a1b])
                        act(SG[:, 0:W], TT[0][:, 0:W], AF.Silu, [ttB[0]], [sgB])
                        if has_s:
                            act(SG[:, W:W + 16], sm1[0][:, :], AF.Silu, [s1B[0]], [sgB])
                        tt("dve", ACTB[:, i, 0:W], SG[:, 0:W], TT[1][:, 0:W], ALU.mult, [sgB, ttB[1]], [aB[i]])
                        if has_s:
                            tt("dve", ACTB[:, i, W:W + 16], SG[:, W:W + 16], sm1[1][:, :], ALU.mult, [sgB, s1B[1]], [aB[i]])

                    for i in range(NPAIR + 1):
                        if i < NPAIR:
                            stage_a(i)
                        if i >= 1:
                            stage_b(i - 1)
                    for dc in range(8):
                        k = cnt["wd"] % 2
                        cnt["wd"] += 1
                        dma("pool", wd[k][:, :, :].rearrange("p a b -> p (a b)"), wdn[l, dc], ("wd", k), wr=[wdB[k]])
                        for s in blk:
                            o, n = SBS[s]
                            lo = o - b0 if s < 4 else W
                            bk, bb = pbu()
                            for i in range(NPAIR):
                                mm(bk[:, 0:n], wd[k][:, i, :], ACTB[:, i, lo:lo + n], i == 0, i == NPAIR - 1,
                                   [wdB[k], aB[i]], [bb], sig=(i == NPAIR - 1))
                            tt("dve", xT[:, dc, o:o + n], bk[:, 0:n], xT[:, dc, o:o + n], ALU.add, [bb, xB[dc][s]], [xB[dc][s]])
                dma("sp", ffn_new[l].rearrange("(c p) k -> p c k", p=128), newh[:, :, :], "st_ffn", rd=nhB)
                emit_stats(pbu)
                S.barrier()
                S.emit(lat=1.0)

        ffn(0)
        if with_dn:
            H["RSP"] = RSP
            H["SQP"] = SQP
            H["emit_stats"] = emit_stats
            dn_hook(H)
        ffn(1)

        with contextlib.ExitStack() as st:
            YO = [sbt(st, f"YO{i}", [128, NT], F32) for i in range(2)]
            yB = [Buf("y0"), Buf("y1")]
            for c in range(8):
                stt("dve", YO[c % 2][:, :], xT[:, c, :], par[:, P_FN + c:P_FN + c + 1], RSP[:, :], ALU.mult, ALU.mult,
                    xB[c] + rsPB + [parB], [yB[c % 2]])
                dma("sp", yT[c * 128:(c + 1) * 128, :], YO[c % 2][:, :], ("st_y", c % 2), rd=[yB[c % 2]])
            S.barrier()
            S.finish("sp")
            S.emit()
    return nc

import contextlib


def dn_hook(H):
    nc, S, sbt = H["nc"], H["S"], H["sbt"]
    xT, par, con, idb, oneb = H["xT"], H["par"], H["con"], H["idb"], H["oneb"]
    xB, parB, conB, idbB = H["xB"], H["parB"], H["conB"], H["idbB"]
    mm, tt, stt, ts, act, cp, memset, dma = H["mm"], H["tt"], H["stt"], H["ts"], H["act"], H["cp"], H["memset"], H["dma"]
    banks, bankB = H["banks"], H["bankB"]
    win, wout, dcstate, sstate = H["win"], H["wout"], H["dcstate"], H["sstate"]
    dnc_new, ssm_p, ssm_s = H["dnc_new"], H["ssm_p"], H["ssm_s"]
    pctr = {"p": 0, "t": 0, "s": 0}

    def pbp():
        i = pctr["p"] % 3
        pctr["p"] += 1
        return banks[i], bankB[i]

    def pbt():
        i = 3 + pctr["t"] % 5
        pctr["t"] += 1
        return banks[i], bankB[i]

    def pbs():
        i = pctr["s"] % 8
        pctr["s"] += 1
        return banks[i], bankB[i]

    idf = con[:, C_ID:C_ID + 128]
    NB = 8
    NLEV = 5
    DB = 256
    with contextlib.ExitStack() as st:
        RSd = H["RSP"]; rsdB = Buf("rsd")
        xD = [[Buf(f"xd{c}_{b}") for b in range(NB + 1)] for c in range(8)]
        newq = sbt(st, "newq", [128, 24, 19], F32); nqB = [Buf(f"nq{i}") for i in range(24)]
        dst = sbt(st, "dst", [128, 24, 16, 3], F32); dstB = Buf("dst")
        QK = [sbt(st, f"QK{p}", [128, 16, DB + 16], BF16) for p in range(2)]
        VT = [sbt(st, f"VT{p}", [128, 8, DB + 16], BF16) for p in range(2)]
        ZS = [sbt(st, f"ZS{p}", [128, 8, DB + 16], BF16) for p in range(2)]
        OG = ZS
        qkB = [[Buf(f"qk{p}_{h}") for h in range(16)] for p in range(2)]
        vB = [[Buf(f"v{p}_{h}") for h in range(8)] for p in range(2)]
        zB = [[Buf(f"z{p}_{h}") for h in range(8)] for p in range(2)]
        ogB = zB
        GA = [sbt(st, f"GA{p}", [8, DB + 16], F32) for p in range(2)]
        GB = [sbt(st, f"GB{p}", [8, DB + 16], F32) for p in range(2)]
        GC = [sbt(st, f"GC{p}", [8, DB], F32) for p in range(2)]
        DK = [sbt(st, f"DK{p}", [8, DB], F32) for p in range(2)]
        gaB = [Buf("ga0"), Buf("ga1")]; gbB = [Buf("gb0"), Buf("gb1")]
        gcB = [Buf("gc0"), Buf("gc1")]; dkB = [Buf("dk0"), Buf("dk1")]
        SC = [sbt(st, f"SC{p}", [128, 2, 24], F32) for p in range(2)]
        EX = [sbt(st, f"EX{p}", [128, 2, 24], F32) for p in range(2)]
        BK = [sbt(st, f"BK{p}", [128, 2, 8], F32) for p in range(2)]
        scB = [Buf("sc0"), Buf("sc1")]; exB = [Buf("ex0"), Buf("ex1")]; bkB = [Buf("bk0"), Buf("bk1")]
        GM = sbt(st, "GM", [8, 128], F32); gmB = Buf("gm")
        nA = sbt(st, "nA", [8, 1], F32); naB = Buf("na")
        S32 = sbt(st, "S32", [128, 8, 128], F32); SBF = sbt(st, "SBF", [128, 8, 128], BF16)
        s32B = [Buf(f"s32_{h}") for h in range(8)]; sbfB = Buf("sbf")
        Otm = sbt(st, "Otm", [128, 8, 128], F32)
        otB = Buf("otm")
        SS = sbt(st, "SS", [128, 8], F32); ssB = Buf("ss")
        EGL = sbt(st, "EGL", [128, 8, 2], F32); eglB = Buf("egl")

        dma("sp", dst[:, :, :, :].rearrange("p a b c -> p (a b c)"), dcstate, "ld_dst", wr=[dstB])
        memset("dve", newq[:, :, 0:3], 0.0, nqB)
        memset("dve", S32[:, :, :], 0.0, s32B)
        memset("dve", SBF[:, :, :], 0.0, [sbfB])
        act(nA[:, :], par[0:8, P_AL:P_AL + 1], AF.Exp, [parB], [naB])
        ts("dve", nA[:, :], nA[:, :], -1.0, None, ALU.mult, None, [naB], [naB])

        wo = [sbt(st, f"wo{i}", [128, 8, 128], BF16) for i in range(2)]
        woB = [Buf("wo0"), Buf("wo1")]
        sa = contextlib.ExitStack()
        NWI = 4
        wi = [sbt(sa, f"wi{i}", [128, 8, 128], BF16) for i in range(NWI)]
        wiB = [Buf(f"wi{i}") for i in range(NWI)]
        Hq = [sbt(sa, f"Hq{i}", [128, 3 + DB], F32) for i in range(2)]
        hqB = [Buf("hq0"), Buf("hq1")]
        TA = sbt(sa, "TA", [128, DB + 16], F32); TB = sbt(sa, "TB", [128, DB + 16], F32)
        taB, tbB = Buf("ta"), Buf("tb")
        XNb1 = sbt(sa, "XNb", [128, 8, DB + 16], BF16)
        XNb = [XNb1, XNb1]
        xnB1 = Buf("xnb")
        xnB = [xnB1, xnB1]
        SQ8 = H["SQP"][:, :, 0:DB]; sq8B = Buf("sq8")
        RQ8 = sbt(sa, "RQ8", [128, 4, DB], F32); rq8B = Buf("rq8")
        SQs = sbt(sa, "SQs", [128, 16, 16], BF16)
        RQs = sbt(sa, "RQs", [128, 16, 16], F32); rqsB = Buf("rqs")
        QD = sbt(sa, "QD", [128, 8, 128], BF16); qdB = Buf("qd")
        KDEC = sbt(sa, "KDEC", [128, 8, 128], BF16); kdB = Buf("kdec")
        QKT = sbt(sa, "QKT", [128, 8, 128], BF16); qktB = Buf("qkt")
        WT = sbt(sa, "WT", [128, 8, 128], BF16); wtB = Buf("wt")
        U = sbt(sa, "U", [128, 8, 128], BF16); uB = Buf("u")
        ON, onB = U, uB
        USOL = sbt(sa, "USOL", [128, 8, 128], F32); usB = Buf("usol")
        ET = [sbt(sa, f"ET{i}", [128, 4, 128], F32) for i in range(2)]; etB = [Buf("et0"), Buf("et1")]
        RA = [sbt(sa, f"Ra{i}", [128, 4, 256], BF16) for i in range(2)]
        raB = [Buf("ra0"), Buf("ra1")]
        AA = [[sbt(sa, f"A{i}_{j}", [128, 4, 128], BF16) for j in range(4)] for i in range(2)]
        AAB = [[Buf(f"A{i}_{j}") for j in range(4)] for i in range(2)]
        GCB = [sbt(sa, f"GCB{i}", [128, 4, 128], F32) for i in range(2)]; gcbB = [Buf("gcb0"), Buf("gcb1")]
        RBt = [GCB[i][:, :, :].bitcast(BF16) for i in range(2)]
        rbB = gcbB
        DF = [sbt(sa, f"DF{i}", [128, 4, 128], F32) for i in range(2)]; dfB = [Buf("df0"), Buf("df1")]
        scr, scrB = USOL, usB

        wic = [0]
        woc = [0]
        hqc = [0]

        def proj_gen(bi):
            p = bi % 2
            last = bi == NB - 1
            c0 = bi * DB
            WB = DB + 16 if last else DB
            for kc in range(8):
                stt("dve", XNb[p][:, kc, 0:DB], xT[:, kc, c0:c0 + DB], par[:, P_N1 + 8 + kc:P_N1 + 9 + kc], RSd[:, c0:c0 + DB],
                    ALU.mult, ALU.mult, [xD[kc][bi], rsdB, parB], [xnB[p]])
                if last:
                    stt("dve", XNb[p][:, kc, DB:DB + 16], xT[:, kc, T:NT], par[:, P_N1 + 8 + kc:P_N1 + 9 + kc], RSd[:, T:NT],
                        ALU.mult, ALU.mult, [xD[kc][NB], rsdB, parB], [xnB[p]])
            yield
            for oc in range(33):
                k = wic[0] % NWI
                wic[0] += 1
                dma("pool", wi[k][:, :, :].rearrange("p a b -> p (a b)"), win[oc], ("wi", k), wr=[wiB[k]])
                rdw = [wiB[k], xnB[p]]
                if oc < 24:
                    if oc < 16:
                        dest, dB, hsel = QK[p], qkB[p][oc], oc
                    else:
                        dest, dB, hsel = VT[p], vB[p][oc - 16], oc - 16
                    Hq_, hqb = Hq[hqc[0] % 2], hqB[hqc[0] % 2]
                    hqc[0] += 1
                    cp("act", Hq_[:, 0:3], newq[:, oc, 0:3], [nqB[oc]], [hqb])
                    bk, bb = pbp()
                    for kc in range(8):
                        mm(bk[:, 0:WB], wi[k][:, kc, :], XNb[p][:, kc, 0:WB], kc == 0, kc == 7, rdw, [bb], sig=(kc == 7))
                    cp("act", Hq_[:, 3:3 + DB], bk[:, 0:DB], [bb], [hqb])
                    if last:
                        cp("act", newq[:, oc, 3:19], bk[:, DB:DB + 16], [bb], [nqB[oc]])
                    cp("act", newq[:, oc, 0:3], Hq_[:, DB:DB + 3], [hqb], [nqB[oc]])
                    w = [par[:, P_DCW + j * 24 + oc:P_DCW + j * 24 + oc + 1] for j in range(4)]
                    ts("dve", TA[:, 0:DB], Hq_[:, 3:3 + DB], w[3], None, ALU.mult, None, [hqb, parB], [taB])
                    stt("dve", TB[:, 0:DB], Hq_[:, 2:2 + DB], w[2], TA[:, 0:DB], ALU.mult, ALU.add, [hqb, parB, taB], [tbB])
                    stt("dve", TA[:, 0:DB], Hq_[:, 1:1 + DB], w[1], TB[:, 0:DB], ALU.mult, ALU.add, [hqb, parB, tbB], [taB])
                    stt("dve", TB[:, 0:DB], Hq_[:, 0:DB], w[0], TA[:, 0:DB], ALU.mult, ALU.add, [hqb, parB, taB], [tbB])
                    if last:
                        ts("dve", TA[:, DB:DB + 16], newq[:, oc, 3:19], w[3], None, ALU.mult, None, [nqB[oc], parB], [taB])
                        stt("dve", TB[:, DB:DB + 16], dst[:, oc, :, 2], w[2], TA[:, DB:DB + 16], ALU.mult, ALU.add, [dstB, parB, taB], [tbB])
                        stt("dve", TA[:, DB:DB + 16], dst[:, oc, :, 1], w[1], TB[:, DB:DB + 16], ALU.mult, ALU.add, [dstB, parB, tbB], [taB])
                        stt("dve", TB[:, DB:DB + 16], dst[:, oc, :, 0], w[0], TA[:, DB:DB + 16], ALU.mult, ALU.add, [dstB, parB, taB], [tbB])
                    act(dest[:, hsel, 0:WB], TB[:, 0:WB], AF.Silu, [tbB], [dB])
                elif oc < 32:
                    h = oc - 24
                    bk, bb = pbp()
                    for kc in range(8):
                        mm(bk[:, 0:WB], wi[k][:, kc, :], XNb[p][:, kc, 0:WB], kc == 0, kc == 7, rdw, [bb], sig=(kc == 7))
                    act(ZS[p][:, h, 0:WB], bk[:, 0:WB], AF.Silu, [bb], [zB[p][h]])
                else:
                    bka, bba = pbp()
                    for kc in range(8):
                        mm(bka[0:8, 0:WB], wi[k][:, kc, 0:8], XNb[p][:, kc, 0:WB], kc == 0, kc == 7, rdw, [bba], sig=(kc == 7))
                    bkb, bbb = pbp()
                    for kc in range(8):
                        mm(bkb[0:8, 0:WB], wi[k][:, kc, 8:16], XNb[p][:, kc, 0:WB], kc == 0, kc == 7, rdw, [bbb], sig=(kc == 7))
                    act(GA[p][:, 0:WB], bka[0:8, 0:WB], AF.Exp, [bba, parB], [gaB[p]], bias=par[0:8, P_DT:P_DT + 1])
                    act(GA[p][:, 0:WB], GA[p][:, 0:WB], AF.Ln, [gaB[p]], [gaB[p]], bias=1.0)
                    ts("dve", GA[p][:, 0:WB], GA[p][:, 0:WB], nA[:, 0:1], None, ALU.mult, None, [gaB[p], naB], [gaB[p]])
                    act(GB[p][:, 0:WB], bkb[0:8, 0:WB], AF.Exp, [bbb], [gbB[p]], scale=-1.0)
                    ts("dve", GB[p][:, 0:WB], GB[p][:, 0:WB], 1.0, None, ALU.add, None, [gbB[p]], [gbB[p]])
                    S.op("dve", lambda e, p=p, WB=WB: e.reciprocal(out=GB[p][:, 0:WB], in_=GB[p][:, 0:WB]), reads=[gbB[p]], writes=[gbB[p]])
                yield
            for g4 in range(4):
                hb_ = qkB[p][g4 * 4:g4 * 4 + 4]
                s1, s2 = (128.0, 128.0 * EPS) if g4 < 2 else (1.0, EPS)
                act(SQ8[:, :, :], QK[p][:, g4 * 4:g4 * 4 + 4, 0:DB], AF.Square, hb_, [sq8B])
                for j2 in range(2):
                    bk, bb = pbp()
                    for j in range(2):
                        mm(bk[:, j * DB:(j + 1) * DB], oneb[:, :], SQ8[:, j2 * 2 + j, :], True, True, [sq8B, idbB], [bb])
                    act(RQ8[:, j2 * 2:j2 * 2 + 2, :], bk[:, :].rearrange("p (a b) -> p a b", a=2), AF.Ln, [bb], [rq8B], scale=s1, bias=s2)
                act(RQ8[:, :, :], RQ8[:, :, :], AF.Exp, [rq8B], [rq8B], scale=-0.5)
                tt("dve", QK[p][:, g4 * 4:g4 * 4 + 4, 0:DB], QK[p][:, g4 * 4:g4 * 4 + 4, 0:DB], RQ8[:, :, :], ALU.mult, hb_ + [rq8B], hb_)
                yield
            if last:
                act(SQs[:, :, :], QK[p][:, :, DB:DB + 16], AF.Square, qkB[p], [rqsB])
                bk, bb = pbp()
                for hd in range(16):
                    mm(bk[:, hd * 16:(hd + 1) * 16], oneb[:, :], SQs[:, hd, :], True, True, [rqsB, idbB], [bb])
                act(RQs[:, 0:8, :], bk[:, 0:128].rearrange("p (a b) -> p a b", a=8), AF.Ln, [bb], [rqsB], scale=128.0, bias=128.0 * EPS)
                act(RQs[:, 8:16, :], bk[:, 128:256].rearrange("p (a b) -> p a b", a=8), AF.Ln, [bb], [rqsB], scale=1.0, bias=EPS)
                act(RQs[:, :, :], RQs[:, :, :], AF.Exp, [rqsB], [rqsB], scale=-0.5)
                tt("dve", QK[p][:, :, DB:DB + 16], QK[p][:, :, DB:DB + 16], RQs[:, :, :], ALU.mult, qkB[p] + [rqsB], qkB[p])
            for ti in range(2):
                tc = ti * 128
                S.op("dve", lambda e, tc=tc: e.tensor_tensor_scan(out=GC[p][:, tc:tc + 128], data0=con[0:8, C_CM:C_CM + 128], data1=GA[p][:, tc:tc + 128],
                                                                  initial=0.0, op0=ALU.mult, op1=ALU.add), reads=[gaB[p], conB], writes=[gcB[p]])
            for ck in range(4):
                stt("dve", DK[p][:, ck * 64:ck * 64 + 64], GC[p][:, ck * 64:ck * 64 + 64], -1.0,
                    GC[p][:, ck * 64 + 63:ck * 64 + 64].to_broadcast([8, 64]), ALU.mult, ALU.add, [gcB[p]], [dkB[p]])
            for ti in range(2):
                tc = ti * 128
                bk, bb = pbp()
                mm(bk[:, 0:8], GC[p][:, tc:tc + 128], con[0:8, C_ID:C_ID + 8], True, True, [gcB[p], conB], [bb])
                mm(bk[:, 8:16], GB[p][:, tc:tc + 128], con[0:8, C_ID:C_ID + 8], True, True, [gbB[p], conB], [bb])
                mm(bk[:, 16:24], DK[p][:, tc:tc + 128], con[0:8, C_ID:C_ID + 8], True, True, [dkB[p], conB], [bb])
                cp("dve", SC[p][:, ti, :], bk[:, 0:24], [bb], [scB[p]])
            act(EX[p][:, :, :], SC[p][:, :, :], AF.Exp, [scB[p]], [exB[p]])
            tt("dve", BK[p][:, :, :], SC[p][:, :, 8:16], EX[p][:, :, 0:8], ALU.mult, [scB[p], exB[p]], [bkB[p]])
            yield

        def b4(ap):
            return ap.unsqueeze(2).to_broadcast([128, 4, 128])

        def tile_gen(bi, ti):
            p = bi % 2
            tc = ti * 128
            QT = QK[p][:, 0:8, tc:tc + 128]
            KT = QK[p][:, 8:16, tc:tc + 128]
            VTt = VT[p][:, :, tc:tc + 128]
            ZSt = ZS[p][:, :, tc:tc + 128]
            GCt = GC[p][:, tc:tc + 128]
            SCt = SC[p][:, ti, :]
            EXt = EX[p][:, ti, :]
            BKt = BK[p][:, ti, :]
            qB_ = qkB[p][0:8]
            kB_ = qkB[p][8:16]
            for hb in range(2):
                hs = [hb * 4 + j for j in range(4)]
                bkg, bbg = pbt()
                for j, h in enumerate(hs):
                    ts("dve", GM[:, :], GCt, con[0:8, C_ID + h:C_ID + h + 1], None, ALU.mult, None, [gcB[p], conB], [gmB])
                    mm(bkg[:, j * 128:(j + 1) * 128], con[0:8, C_ONE:C_ONE + 128], GM[:, :], True, True, [gmB, conB], [bbg])
                cp("act", GCB[hb][:, :, :].rearrange("p a b -> p (a b)"), bkg[:, :], [bbg], [gcbB[hb]])
                act(EGL[:, hb * 4:hb * 4 + 4, 0], GCB[hb][:, :, 63], AF.Exp, [gcbB[hb]], [eglB])
                act(EGL[:, hb * 4:hb * 4 + 4, 1], GCB[hb][:, :, 127], AF.Exp, [gcbB[hb]], [eglB])
                act(DF[hb][:, :, :], GCB[hb][:, :, :], AF.Exp, [gcbB[hb]], [dfB[hb]])
                tt("dve", QD[:, hb * 4:hb * 4 + 4, :], QT[:, hb * 4:hb * 4 + 4, :], DF[hb][:, :, :], ALU.mult, [dfB[hb]] + [qB_[h] for h in hs], [qdB])
            yield
            for hb in range(2):
                hs = [hb * 4 + j for j in range(4)]
                bkk, bbk = pbt()
                bkq, bbq = pbt()
                bkt, bbt = pbt()
                for j, h in enumerate(hs):
                    sl = slice(j * 128, (j + 1) * 128)
                    mm(bkk[:, sl], KT[:, h, :], KT[:, h, :], True, True, [kB_[h]], [bbk])
                    mm(bkq[:, sl], KT[:, h, :], QT[:, h, :], True, True, [kB_[h], qB_[h]], [bbq])
                    mm(bkt[:, sl], KT[:, h, :], idb[:, :], True, True, [kB_[h], idbB], [bbt])
                k3 = bkt[:, :].rearrange("p (a b) -> p a b", a=4)
                tt("dve", KDEC[:, hb * 4:hb * 4 + 4, :], k3, b4(EXt[:, 16 + hb * 4:20 + hb * 4]), ALU.mult, [bbt, exB[p]], [kdB])
                tt("dve", RA[hb][:, :, 128:256], k3, b4(BKt[:, hb * 4:hb * 4 + 4]), ALU.mult, [bbt, bkB[p]], [raB[hb]])
                tt("dve", DF[hb][:, :, :], GCB[hb][:, :, :], b4(SCt[:, hb * 4:hb * 4 + 4]), ALU.subtract, [gcbB[hb], scB[p], dfB[hb]], [dfB[hb]])
                mL = con[:, C_ML:C_ML + 128].unsqueeze(1).to_broadcast([128, 4, 128])
                mU = con[:, C_MU:C_MU + 128].unsqueeze(1).to_broadcast([128, 4, 128])
                tt("dve", ET[hb][:, :, :], DF[hb][:, :, :], mU, ALU.add, [dfB[hb], conB], [etB[hb]])
                tt("dve", DF[hb][:, :, :], DF[hb][:, :, :], mL, ALU.add, [dfB[hb], conB], [dfB[hb]])
                act(DF[hb][:, :, :], DF[hb][:, :, :], AF.Exp, [dfB[hb]], [dfB[hb]], scale=-1.0)
                act(ET[hb][:, :, :], ET[hb][:, :, :], AF.Exp, [etB[hb]], [etB[hb]])
                tt("dve", DF[hb][:, :, :], DF[hb][:, :, :], b4(SCt[:, 8 + hb * 4:12 + hb * 4]), ALU.mult, [dfB[hb], scB[p]], [dfB[hb]])
                A = AA[hb]
                AB = AAB[hb]
                tt("dve", A[0][:, :, :], bkk[:, :].rearrange("p (a b) -> p a b", a=4), DF[hb][:, :, :], ALU.mult, [bbk, dfB[hb]], [AB[0]])
                tt("dve", QKT[:, hb * 4:hb * 4 + 4, :], bkq[:, :].rearrange("p (a b) -> p a b", a=4), ET[hb][:, :, :], ALU.mult, [bbq, etB[hb]], [qktB])
                bvt, bbv = pbt()
                bkn, bbn = pbt()
                for j, h in enumerate(hs):
                    sl = slice(j * 128, (j + 1) * 128)
                    mm(bvt[:, sl], VTt[:, h, :], idb[:, :], True, True, [vB[p][h], idbB], [bbv])
                for j in range(4):
                    mm(bkn[:, j * 128:(j + 1) * 128], A[0][:, j, :], idb[:, :], True, True, [AB[0], idbB], [bbn])
                v3 = bvt[:, :].rearrange("p (a b) -> p a b", a=4)
                tt("dve", RA[hb][:, :, 0:128], v3, b4(SCt[:, 8 + hb * 4:12 + hb * 4]), ALU.mult, [bbv, scB[p]], [raB[hb]])
                cp("act", A[1][:, :, :].rearrange("p a b -> p (a b)"), bkn[:, :], [bbn], [AB[1]])
                yield
            Rc = [RA[0], RA[1]]; Rn = [RBt[0], RBt[1]]
            rcB = [raB[0], raB[1]]; rnB = [rbB[0], rbB[1]]
            idx = [[0, 1, 2, 3], [0, 1, 2, 3]]
            for lev in range(NLEV):
                for hb in range(2):
                    A, AB = AA[hb], AAB[hb]
                    ak, akT, an, anT = idx[hb]
                    pbk = [pbt(), pbt()]
                    for j in range(4):
                        bk_, bb_ = pbk[j // 2]
                        mm(bk_[:, (j % 2) * 256:(j % 2) * 256 + 256], A[akT][:, j, :], Rc[hb][:, j, :], True, True, [AB[akT], rcB[hb]], [bb_])
                    for q2 in range(2):
                        bk_, bb_ = pbk[q2]
                        tt("dve", Rn[hb][:, 2 * q2:2 * q2 + 2, :], Rc[hb][:, 2 * q2:2 * q2 + 2, :], bk_[:, :].rearrange("p (a b) -> p a b", a=2),
                           ALU.subtract if lev == 0 else ALU.add, [rcB[hb], bb_], [rnB[hb]])
                    Rc[hb], Rn[hb], rcB[hb], rnB[hb] = Rn[hb], Rc[hb], rnB[hb], rcB[hb]
                    if lev < NLEV - 1:
                        b1, bb1 = pbt()
                        for j in range(4):
                            sl = slice(j * 128, (j + 1) * 128)
                            mm(b1[:, sl], A[akT][:, j, :], A[ak][:, j, :], True, True, [AB[ak], AB[akT]], [bb1])
                        cp("act", A[an][:, :, :].rearrange("p a b -> p (a b)"), b1[:, :], [bb1], [AB[an]])
                        b2, bb2 = pbt()
                        for j in range(4):
                            sl = slice(j * 128, (j + 1) * 128)
                            mm(b2[:, sl], A[ak][:, j, :], A[akT][:, j, :], True, True, [AB[ak], AB[akT]], [bb2])
                        cp("act", A[anT][:, :, :].rearrange("p a b -> p (a b)"), b2[:, :], [bb2], [AB[anT]])
                        idx[hb] = [an, anT, ak, akT]
                    yield
            for hb in range(2):
                cp("dve", USOL[:, hb * 4:hb * 4 + 4, :], Rc[hb][:, :, 0:128], [rcB[hb]], [usB])
                bkw, bbw = pbt()
                for j in range(4):
                    mm(bkw[:, j * 128:(j + 1) * 128], Rc[hb][:, j, 128:256], idb[:, :], True, True, [rcB[hb], idbB], [bbw])
                cp("act", WT[:, hb * 4:hb * 4 + 4, :].rearrange("p a b -> p (a b)"), bkw[:, :], [bbw], [wtB])
            yield
            for ck in range(2):
                r0 = ck * 64
                rs_ = slice(r0, r0 + 64)
                ws = [pbt(), pbt()]
                for h in range(8):
                    bk_, bb_ = ws[h // 4]
                    mm(bk_[:, (h % 4) * 128:(h % 4) * 128 + 128], WT[:, h, :], SBF[:, h, :], True, True, [wtB, sbfB], [bb_])
                for q2 in range(2):
                    bk_, bb_ = ws[q2]
                    tt("dve", U[rs_, q2 * 4:q2 * 4 + 4, :], USOL[rs_, q2 * 4:q2 * 4 + 4, :], bk_[rs_, :].rearrange("p (a b) -> p a b", a=4),
                       ALU.subtract, [usB, bb_], [uB])
                po = [pbt(), pbt()]
                for h in range(8):
                    bk_, bb_ = po[h // 4]
                    sl = slice((h % 4) * 128, (h % 4) * 128 + 128)
                    mm(bk_[:, sl], QD[:, h, :], SBF[:, h, :], True, False, [qdB, sbfB], [bb_])
                    mm(bk_[:, sl], QKT[rs_, h, :], U[rs_, h, :], False, True, [qktB, uB], [bb_])
                for q2 in range(2):
                    bk_, bb_ = po[q2]
                    cp("act", Otm[rs_, q2 * 4:q2 * 4 + 4, :].rearrange("p a b -> p (a b)"), bk_[rs_, :], [bb_], [otB])
                pu = [pbt(), pbt()]
                for h in range(8):
                    bk_, bb_ = pu[h // 4]
                    sl = slice((h % 4) * 128, (h % 4) * 128 + 128)
                    mm(bk_[:, sl], KDEC[rs_, h, :], U[rs_, h, :], True, True, [kdB, uB], [bb_])
                for h in range(8):
                    bk_, bb_ = pu[h // 4]
                    sl = slice((h % 4) * 128, (h % 4) * 128 + 128)
                    stt("dve", S32[:, h, :], S32[:, h, :], EGL[:, h, ck:ck + 1], bk_[:, sl], ALU.mult, ALU.add, [s32B[h], eglB, bb_], [s32B[h]])
                cp("act", SBF[:, :, :], S32[:, :, :], s32B, [sbfB])
                yield
            tt("dve", scr[:, :, :], Otm[:, :, :], Otm[:, :, :], ALU.mult, [otB], [scrB])
            S.op("dve", lambda e: e.tensor_reduce(out=SS[:, :], in_=scr[:, :, :], axis=AX.X, op=ALU.add), reads=[scrB], writes=[ssB])
            ts("dve", SS[:, :], SS[:, :], 1.0 / 128, EPS, ALU.mult, ALU.add, [ssB], [ssB])
            act(SS[:, :], SS[:, :], AF.Sqrt, [ssB], [ssB])
            S.op("dve", lambda e: e.reciprocal(out=SS[:, :], in_=SS[:, :]), reads=[ssB], writes=[ssB])
            tt("dve", ON[:, :, :], Otm[:, :, :], SS[:, :].unsqueeze(2).to_broadcast([128, 8, 128]), ALU.mult, [otB, ssB], [onB])
            for hh in range(2):
                bk, bb = pbt()
                for j in range(4):
                    h = hh * 4 + j
                    mm(bk[:, j * 128:(j + 1) * 128], ON[:, h, :], idb[:, :], True, True, [onB, idbB], [bb])
                for j in range(4):
                    h = hh * 4 + j
                    stt("dve", ZSt[:, h, :], bk[:, j * 128:(j + 1) * 128], par[:, P_ONW:P_ONW + 1], ZSt[:, h, :],
                        ALU.mult, ALU.mult, [bb, parB, zB[p][h]], [zB[p][h]])
            yield

        def outproj_gen(bi):
            p = bi % 2
            c0 = bi * DB
            for dc in range(8):
                k = woc[0] % 2
                woc[0] += 1
                dma("pool", wo[k][:, :, :].rearrange("p a b -> p (a b)"), wout[dc], ("wo", k), wr=[woB[k]])
                bk, bb = pbt()
                for hc in range(8):
                    mm(bk[:, 0:DB], wo[k][:, hc, :], OG[p][:, hc, 0:DB], hc == 0, hc == 7, [woB[k], ogB[p][hc]], [bb], sig=(hc == 7))
                tt("dve", xT[:, dc, c0:c0 + DB], bk[:, 0:DB], xT[:, dc, c0:c0 + DB], ALU.add, [bb, xD[dc][bi]], [xD[dc][bi]])
                if dc % 2 == 1:
                    yield

        def block_gen(bi):
            yield from tile_gen(bi, 0)
            yield from tile_gen(bi, 1)
            yield from outproj_gen(bi)

        def drive(gens):
            gens = list(gens)
            while gens:
                for g in list(gens):
                    try:
                        next(g)
                    except StopIteration:
                        gens.remove(g)

        drive([proj_gen(0)])
        for bi in range(NB):
            gs = [block_gen(bi)]
            if bi + 1 < NB:
                gs.append(proj_gen(bi + 1))
            drive(gs)

        S.barrier()
        S.emit(lat=0.2, cp=True)
        sa.close()
        p = (NB - 1) % 2
        QT = QK[p][:, 0:8, :]
        KT = QK[p][:, 8:16, :]
        pb = pbs
        dma("sp", ssm_p.rearrange("h d v -> d h v"), S32[:, :, :], "st_ssmp", rd=s32B)
        dma("sp", dnc_new.rearrange("(c p) k -> p c k", p=128), newq[:, :, :], "st_dnc", rd=nqB)
        with contextlib.ExitStack() as sb_:
            K32 = sbt(sb_, "K32", [128, 8, 16], F32); Q32 = sbt(sb_, "Q32", [128, 8, 16], F32); V32 = sbt(sb_, "V32", [128, 8, 16], F32)
            k32B = Buf("k32")
            KM = sbt(sb_, "KM", [128, 8, 16, 16], BF16); QM = sbt(sb_, "QM", [128, 8, 16, 16], BF16)
            kmB, qmB = Buf("km"), Buf("qm")
            EGs = sbt(sb_, "EGs", [8, 16], F32); egsB = Buf("egs")
            SCS = sbt(sb_, "SCS", [16, 16], F32); scsB = Buf("scs")
            VTM = sbt(sb_, "VTM", [16, 8, 128], F32); KTM = sbt(sb_, "KTM", [16, 8, 128], BF16)
            vtmB, ktmB = Buf("vtm"), Buf("ktm")
            EGD = sbt(sb_, "EGD", [16, 16, 8], F32); egdB = Buf("egd")
            EGS = sbt(sb_, "EGS", [128, 16, 8], F32); egSB = Buf("egS")
            Us = sbt(sb_, "Us", [16, 8, 128], F32); usB2 = Buf("us")
            UM = sbt(sb_, "UM", [16, 8, 128], BF16); umB = Buf("um")
            SbH = [sbt(sb_, f"SbH{i}", [128, 8, 128], BF16) for i in range(2)]
            sbhB = [Buf("sbh0"), Buf("sbh1")]
            SnH = [sbt(sb_, f"SnH{i}", [128, 8, 128], BF16) for i in range(2)]
            snhB = [Buf("snh0"), Buf("snh1")]
            NSB = 3
            Sb = [sbt(sb_, f"Sb{i}", [128, 8, 128], F32) for i in range(NSB)]
            sbB = [Buf(f"sb{i}") for i in range(NSB)]
            Sn = [sbt(sb_, f"Sn{i}", [128, 8, 128], F32) for i in range(3)]
            snB = [Buf("sn0"), Buf("sn1"), Buf("sn2")]
            scr2 = sbt(sb_, "scr2", [16, 8, 128], F32); scr2B = Buf("scr2")
            ZL = sbt(sb_, "ZL", [128, 16], F32); zlB = Buf("zl")
            ON = sbt(sb_, "ONs", [16, 8, 128], BF16); onB = Buf("ons")
            memset("dve", ZL[:, :], 0.0, [zlB])
            cp("dve", K32[:, :, :], KT[:, :, DB:DB + 16], qkB[p], [k32B])
            cp("dve", Q32[:, :, :], QT[:, :, DB:DB + 16], qkB[p], [k32B])
            cp("dve", V32[:, :, :], VT[p][:, :, DB:DB + 16], vB[p], [k32B])
            i16 = con[:, C_I16:C_I16 + 256].rearrange("p (a b) -> p a b", a=16)
            for h in range(8):
                tt("dve", KM[:, h, :, :], K32[:, h, :].unsqueeze(2).to_broadcast([128, 16, 16]), i16, ALU.mult, [k32B, conB], [kmB])
                tt("dve", QM[:, h, :, :], Q32[:, h, :].unsqueeze(2).to_broadcast([128, 16, 16]), i16, ALU.mult, [k32B, conB], [qmB])
            act(EGs[:, :], GA[p][:, DB:DB + 16], AF.Exp, [gaB[p]], [egsB])
            bk, bb = pb()
            mm(bk[0:16, 0:8], EGs[:, :], con[0:8, C_ID:C_ID + 8], True, True, [egsB, conB], [bb])
            mm(bk[0:16, 8:16], GB[p][:, DB:DB + 16], con[0:8, C_ID:C_ID + 8], True, True, [gbB[p], conB], [bb])
            cp("dve", SCS[:, :], bk[0:16, 0:16], [bb], [scsB])
            for (src, dstt, dB_) in ((V32, VTM, vtmB), (K32, KTM, ktmB)):
                for hh in range(2):
                    bk, bb = pb()
                    for j in range(4):
                        mm(bk[0:16, j * 128:(j + 1) * 128], src[:, hh * 4 + j, :], idf, True, True, [k32B, conB], [bb])
                    cp("dve", dstt[:, hh * 4:hh * 4 + 4, :].rearrange("p a b -> p (a b)"), bk[0:16, :], [bb], [dB_])
            tt("dve", EGD[:, :, :], SCS[:, 0:8].unsqueeze(1).to_broadcast([16, 16, 8]),
               con[0:16, C_ID:C_ID + 16].unsqueeze(2).to_broadcast([16, 16, 8]), ALU.mult, [scsB, conB], [egdB])
            bk, bb = pb()
            mm(bk[:, 0:128], con[0:16, C_ONE:C_ONE + 128], EGD[:, :, :].rearrange("p a b -> p (a b)"), True, True, [egdB, conB], [bb])
            cp("dve", EGS[:, :, :].rearrange("p a b -> p (a b)"), bk[:, 0:128], [bb], [egSB])
            KSA, ksaB = scr2, scr2B
            OSA, osaB = Otm[0:16, :, :], otB
            memset("dve", KSA[:, :, :], 0.0, [ksaB])
            memset("dve", OSA[:, :, :], 0.0, [osaB])
            for b in range(16):
                dma("sp", Sb[b % NSB][:, :, :].rearrange("p a b -> p (a b)"), sstate[b], ("ld_S", b % NSB), wr=[sbB[b % NSB]])
                cp("act", SbH[b % 2][:, :, :], Sb[b % NSB][:, :, :], [sbB[b % NSB]], [sbhB[b % 2]])
                pk = [pb(), pb()]
                for h in range(8):
                    bk_, bb_ = pk[h // 4]
                    mm(bk_[0:16, (h % 4) * 128:(h % 4) * 128 + 128], KM[:, h, b, :], SbH[b % 2][:, h, :], True, True, [kmB, sbhB[b % 2]], [bb_])
                for q2 in range(2):
                    bk_, bb_ = pk[q2]
                    tt("dve", KSA[:, q2 * 4:q2 * 4 + 4, :], KSA[:, q2 * 4:q2 * 4 + 4, :], bk_[0:16, :].rearrange("p (a b) -> p a b", a=4),
                       ALU.add, [ksaB, bb_], [ksaB])
            tt("dve", scr2[:, :, :], KSA[:, :, :], SCS[:, 0:8].unsqueeze(2).to_broadcast([16, 8, 128]), ALU.mult, [ksaB, scsB], [scr2B])
            tt("dve", scr2[:, :, :], VTM[:, :, :], scr2[:, :, :], ALU.subtract, [vtmB, scr2B], [scr2B])
            tt("dve", Us[:, :, :], scr2[:, :, :], SCS[:, 8:16].unsqueeze(2).to_broadcast([16, 8, 128]), ALU.mult, [scr2B, scsB], [usB2])
            for b in range(16):
                sl_ = b % 2
                sl3 = (16 + b) % NSB
                sn3 = b % 3
                dma("sp", Sb[sl3][:, :, :].rearrange("p a b -> p (a b)"), sstate[b], ("ld_S", sl3), wr=[sbB[sl3]])
                ts("dve", UM[:, :, :], Us[:, :, :], con[0:16, C_ID + b:C_ID + b + 1], None, ALU.mult, None, [usB2, conB], [umB])
                pu = [pb(), pb()]
                for h in range(8):
                    bk_, bb_ = pu[h // 4]
                    mm(bk_[:, (h % 4) * 128:(h % 4) * 128 + 128], KTM[:, h, :], UM[:, h, :], True, True, [ktmB, umB], [bb_])
                for h in range(8):
                    bk_, bb_ = pu[h // 4]
                    stt("dve", Sn[sn3][:, h, :], Sb[sl3][:, h, :], EGS[:, b, h:h + 1], bk_[:, (h % 4) * 128:(h % 4) * 128 + 128],
                        ALU.mult, ALU.add, [sbB[sl3], egSB, bb_], [snB[sn3]])
                dma("sp", ssm_s[b].rearrange("h d v -> d h v"), Sn[sn3][:, :, :], ("st_S", sn3), rd=[snB[sn3]])
                cp("act", SnH[sl_][:, :, :], Sn[sn3][:, :, :], [snB[sn3]], [snhB[sl_]])
                po_ = [pb(), pb()]
                for h in range(8):
                    bk_, bb_ = po_[h // 4]
                    mm(bk_[0:16, (h % 4) * 128:(h % 4) * 128 + 128], QM[:, h, b, :], SnH[sl_][:, h, :], True, True, [qmB, snhB[sl_]], [bb_])
                for q2 in range(2):
                    bk_, bb_ = po_[q2]
                    tt("dve", OSA[:, q2 * 4:q2 * 4 + 4, :], OSA[:, q2 * 4:q2 * 4 + 4, :], bk_[0:16, :].rearrange("p (a b) -> p a b", a=4),
                       ALU.add, [osaB, bb_], [osaB])
            R_ = 16
            tt("dve", scr2[0:R_, :, :], Otm[0:R_, :, :], Otm[0:R_, :, :], ALU.mult, [otB], [scr2B])
            S.op("dve", lambda e: e.tensor_reduce(out=SS[0:16, :], in_=scr2[0:16, :, :], axis=AX.X, op=ALU.add), reads=[scr2B], writes=[ssB])
            ts("dve", SS[0:R_, :], SS[0:R_, :], 1.0 / 128, EPS, ALU.mult, ALU.add, [ssB], [ssB])
            act(SS[0:R_, :], SS[0:R_, :], AF.Sqrt, [ssB], [ssB])
            S.op("dve", lambda e: e.reciprocal(out=SS[0:16, :], in_=SS[0:16, :]), reads=[ssB], writes=[ssB])
            tt("dve", ON[0:R_, :, :], Otm[0:R_, :, :], SS[0:R_, :].unsqueeze(2).to_broadcast([R_, 8, 128]), ALU.mult, [otB, ssB], [onB])
            for hh in range(2):
                bk, bb = pb()
                for j in range(4):
                    h = hh * 4 + j
                    mm(bk[:, j * 128:j * 128 + R_], ON[0:R_, h, :], idb[0:R_, 0:R_], True, True, [onB, idbB], [bb])
                for j in range(4):
                    h = hh * 4 + j
                    stt("dve", OG[p][:, h, DB:DB + 16], bk[:, j * 128:j * 128 + R_], par[:, P_ONW:P_ONW + 1], ZS[p][:, h, DB:DB + 16],
                        ALU.mult, ALU.mult, [bb, parB, zB[p][h]], [ogB[p][h]])
            for dc in range(8):
                k = woc[0] % 2
                woc[0] += 1
                dma("pool", wo[k][:, :, :].rearrange("p a b -> p (a b)"), wout[dc], ("wo", k), wr=[woB[k]])
                bk, bb = pb()
                for hc in range(8):
                    mm(bk[:, 0:16], wo[k][:, hc, :], OG[p][:, hc, DB:DB + 16], hc == 0, hc == 7, [woB[k], ogB[p][hc]], [bb], sig=(hc == 7))
                tt("dve", xT[:, dc, T:NT], bk[:, 0:16], xT[:, dc, T:NT], ALU.add, [bb, xD[dc][NB]], [xD[dc][NB]])
            S.barrier()
            S.emit(lat=1.0)
        S.barrier()
        S.emit()

import numpy as np


def make_consts():
    con = np.zeros((128, NCON), np.float32)
    con[:, C_ID:C_ID + 128] = np.eye(128, dtype=np.float32)
    i = np.arange(128)[:, None]
    j = np.arange(128)[None, :]
    same = (i // 64) == (j // 64)
    con[:, C_ML:C_ML + 128] = np.where(same & (i > j), 0.0, 30000.0).astype(np.float32)
    con[:, C_MU:C_MU + 128] = np.where(same & (j >= i), 0.0, -30000.0).astype(np.float32)
    t = np.arange(128)
    con[:, C_CM:C_CM + 128] = (t % 64 != 0).astype(np.float32)[None, :]
    for g in range(4):
        w = 2 ** (g + 1)
        pos = np.arange(16)
        con[:, C_RT + g * 16:C_RT + g * 16 + 16] = (w / np.minimum(w, pos + 1)).astype(np.float32)[None, :]
    con[:, C_ONE:C_ONE + 128] = 1.0
    con[:, C_I16:C_I16 + 256] = np.eye(16, dtype=np.float32).reshape(1, 256)
    return con


def colvec(v, n):
    return np.ascontiguousarray(v.reshape(n, 128).T)


def make_shared(inp):
    f = lambda k: np.asarray(inp[k], np.float32)
    sh = {}
    wu = f("ffn_w_up").reshape(2, 8, 128, 2, NPAIR, 128)
    sh["wup"] = np.ascontiguousarray(wu.transpose(0, 4, 2, 1, 3, 5)).reshape(2, NPAIR, 128, 2048)
    wd = f("ffn_w_down").reshape(2, NPAIR, 128, 8, 128)
    sh["wdn"] = np.ascontiguousarray(wd.transpose(0, 3, 2, 1, 4)).reshape(2, 8, 128, NPAIR * 128)
    wi = np.zeros((1024, 33 * 128), np.float32)
    wi[:, :4112] = f("dn_w_in")[0]
    wi = wi.reshape(8, 128, 33, 128)
    sh["win"] = np.ascontiguousarray(wi.transpose(2, 1, 0, 3)).reshape(33, 128, 1024)
    wo = f("dn_w_out")[0].reshape(8, 128, 8, 128)
    sh["wout"] = np.ascontiguousarray(wo.transpose(2, 1, 0, 3)).reshape(8, 128, 1024)
    pw = f("pool_w")[0].reshape(4, 2, 128, 256)
    sh["poolw"] = np.ascontiguousarray(pw.transpose(2, 0, 1, 3)).reshape(128, 2048)
    par = np.zeros((128, NPAR), np.float32)
    for l in range(2):
        par[:, P_N1 + 8 * l:P_N1 + 8 * l + 8] = colvec(f("norm1_w")[l], 8)
        par[:, P_N2 + 8 * l:P_N2 + 8 * l + 8] = colvec(f("norm2_w")[l], 8)
        for jj in range(3):
            par[:, P_FCW + (l * 3 + jj) * 44:P_FCW + (l * 3 + jj) * 44 + 44] = colvec(f("ffn_conv_w")[l, jj], 44)
        par[:, P_FCB + l * 44:P_FCB + l * 44 + 44] = colvec(f("ffn_conv_b")[l], 44)
    par[:, P_FN:P_FN + 8] = colvec(f("final_norm_w"), 8)
    par[:, P_PS:P_PS + 8] = colvec(f("pool_scale")[0], 8)
    for jj in range(4):
        par[:, P_DCW + jj * 24:P_DCW + jj * 24 + 24] = colvec(f("dn_conv_w")[0, jj], 24)
    par[:, P_ONW] = f("dn_o_norm_w")[0]
    par[0:8, P_AL] = f("dn_a_log")[0]
    par[0:8, P_DT] = f("dn_dt_bias")[0]
    sh["params"] = par
    sh["consts"] = make_consts()
    return sh


def make_core(inp, c):
    f = lambda k: np.asarray(inp[k], np.float32)
    m = {}
    bs = slice(16 * c, 16 * c + 16)
    m["xin"] = np.ascontiguousarray(np.concatenate([f("x_prompt")[c].T, f("x_sample")[bs, 0, :].T], axis=1))
    ps = f("state_pool_buf")[0, bs].reshape(16, 15, 8, 128)
    m["pstate"] = np.ascontiguousarray(ps.transpose(3, 2, 0, 1)).reshape(128, 8 * 16 * 15)
    fs = f("state_ffn_conv")[:, bs].reshape(2, 16, 2, 44, 128)
    m["fstate"] = np.ascontiguousarray(fs.transpose(0, 4, 3, 1, 2)).reshape(2, 128, 44 * 32)
    ds = f("state_dn_conv")[0, bs].reshape(16, 3, 24, 128)
    m["dcstate"] = np.ascontiguousarray(ds.transpose(3, 2, 0, 1)).reshape(128, 24 * 48)
    ss = f("state_dn_ssm")[0, bs]
    m["sstate"] = np.ascontiguousarray(ss.transpose(0, 2, 1, 3)).reshape(16, 128, 1024)
    m["pool_raw"] = np.ascontiguousarray(f("state_pool_buf")[0, bs].reshape(16, 15 * 1024))
    m["f_raw"] = np.ascontiguousarray(f("state_ffn_conv")[:, bs].reshape(2, 16, 2 * 5632))
    m["dc_raw"] = np.ascontiguousarray(f("state_dn_conv")[0, bs].reshape(16, 3 * 3072))
    return m


def assemble(results, ncores=8):
    y_p = np.zeros((8, T, D), np.float32)
    y_s = np.zeros((128, 1, D), np.float32)
    pool_p = np.zeros((1, 8, 15, D), np.float32)
    pool_s = np.zeros((1, 128, 15, D), np.float32)
    dnc_p = np.zeros((1, 8, 3, 3072), np.float32)
    dnc_s = np.zeros((1, 128, 3, 3072), np.float32)
    dns_p = np.zeros((1, 8, 8, 128, 128), np.float32)
    dns_s = np.zeros((1, 128, 8, 128, 128), np.float32)
    ffn_p = np.zeros((2, 8, 2, 5632), np.float32)
    ffn_s = np.zeros((2, 128, 2, 5632), np.float32)
    for c in range(ncores):
        r = results[c]
        bs = slice(16 * c, 16 * c + 16)
        y_p[c] = r["yT"][:, :T].T
        y_s[bs, 0] = r["yT"][:, T:].T
        pool_p[0, c] = r["pool_new"][:, 0:15].T
        pool_s[0, bs, 0:14] = r["pool_old"].reshape(16, 14, D)
        pool_s[0, bs, 14] = r["pool_new"][:, 15:31].T
        dnc_p[0, c] = r["dnc_new"][:, 0:3].T
        dnc_s[0, bs, 0:2] = r["dnc_old"].reshape(16, 2, 3072)
        dnc_s[0, bs, 2] = r["dnc_new"][:, 3:19].T
        dns_p[0, c] = r["ssm_p"]
        dns_s[0, bs] = r["ssm_s"]
        for l in range(2):
            ffn_p[l, c] = r["ffn_new"][l][:, 0:2].T
            ffn_s[l, bs, 0] = r["ffn_old"][l]
            ffn_s[l, bs, 1] = r["ffn_new"][l][:, 2:18].T
    return (y_p, y_s, pool_p, pool_s, dnc_p, dnc_s, dns_p, dns_s, ffn_p, ffn_s)


_NC_CACHE = {}


def kernel(**inputs):
    inp = {k: np.asarray(v) for k, v in inputs.items()}
    if "nc" not in _NC_CACHE:
        _NC_CACHE["nc"] = build_program(True, dn_hook)
    nc = _NC_CACHE["nc"]
    sh = make_shared(inp)
    in_maps = []
    for c in range(8):
        m = dict(sh)
        m.update(make_core(inp, c))
        in_maps.append(m)
    res = run_bass_kernel_spmd(nc, in_maps, core_ids=list(range(8)))
    return assemble(res.results, ncores=8)
```
